# Optimizing a Trainium2 kernel written in Bass

```python
import jax, jax.numpy as jnp
from jax import lax
import numpy as np

D_MODEL = 1024
BATCH = 4
SEQ = 8192
DEPTH = 1

D_MIX = D_MODEL
D_CONV = D_MIX // 2
D_ATTN = D_MIX - D_CONV
ATTN_HEAD_DIM = 64
N_ATTN_HEADS = D_ATTN // ATTN_HEAD_DIM
CONV_WIDTH = 31
D_FF = -(-8 * D_MODEL // (3 * 256)) * 256
Q_BLOCK = 128
EPS = 1e-6
D_IN = 2 * D_CONV + 3 * D_ATTN + N_ATTN_HEADS

kernel_name = "hymba_conformer_fox_sandwich"


def rmsnorm(x, g):
    xf = x.astype(jnp.float32)
    y = xf * lax.rsqrt(jnp.mean(xf * xf, axis=-1, keepdims=True) + EPS)
    return (y * g.astype(jnp.float32)).astype(x.dtype)


def layernorm(x, g, b):
    xf = x.astype(jnp.float32)
    mu = jnp.mean(xf, axis=-1, keepdims=True)
    var = jnp.mean(jnp.square(xf - mu), axis=-1, keepdims=True)
    y = (xf - mu) * lax.rsqrt(var + EPS)
    return (y * g.astype(jnp.float32) + b.astype(jnp.float32)).astype(x.dtype)


def causal_depthwise_conv(u, w, b):
    c = u.shape[-1]
    y = lax.conv_general_dilated(
        u, w[:, None, :].astype(u.dtype), window_strides=(1,),
        padding=[(CONV_WIDTH - 1, 0)],
        dimension_numbers=("NWC", "WIO", "NWC"),
        feature_group_count=c)
    return y + b.astype(u.dtype)


def forgetting_attention(q, k, v, log_f):
    b_, h_, s_, dh = q.shape
    c = jnp.cumsum(log_f, axis=-1)
    kpos = jnp.arange(s_)
    scale = dh ** -0.5

    def one_block(i):
        start = i * Q_BLOCK
        qb = lax.dynamic_slice_in_dim(q, start, Q_BLOCK, axis=2)
        cb = lax.dynamic_slice_in_dim(c, start, Q_BLOCK, axis=2)
        logits = jnp.einsum("bhqd,bhkd->bhqk", qb, k,
                            preferred_element_type=jnp.float32) * scale
        logits = logits + cb[..., :, None] - c[..., None, :]
        qpos = start + jnp.arange(Q_BLOCK)
        causal = kpos[None, :] <= qpos[:, None]
        logits = jnp.where(causal, logits, -jnp.inf)
        p = jax.nn.softmax(logits, axis=-1).astype(v.dtype)
        return jnp.einsum("bhqk,bhkd->bhqd", p, v)

    out = lax.map(one_block, jnp.arange(s_ // Q_BLOCK))
    return jnp.moveaxis(out, 0, 2).reshape(b_, h_, s_, dh)


def setup_inputs(seed: int = 0) -> dict:
    key = jax.random.key(seed)
    ks = jax.random.split(key, 16)
    f32 = jnp.float32

    def nrm(k, shape, scale):
        return jax.random.normal(k, shape, f32) * scale

    def gain(k, n):
        return 1.0 + nrm(k, (DEPTH, n), 0.02)

    return {
        "x": jax.random.normal(ks[0], (BATCH, SEQ, D_MODEL), f32),
        "g_mix_pre": gain(ks[1], D_MODEL),
        "w_in": nrm(ks[2], (DEPTH, D_MODEL, D_IN), D_MODEL ** -0.5),
        "b_forget": 3.0 + nrm(ks[3], (DEPTH, N_ATTN_HEADS), 0.5),
        "conv_w": nrm(ks[4], (DEPTH, CONV_WIDTH, D_CONV), CONV_WIDTH ** -0.5),
        "conv_b": nrm(ks[5], (DEPTH, D_CONV), 0.02),
        "conv_ln_g": gain(ks[6], D_CONV),
        "conv_ln_b": nrm(ks[7], (DEPTH, D_CONV), 0.02),
        "w_out": nrm(ks[8], (DEPTH, D_MIX, D_MODEL), D_MIX ** -0.5),
        "g_mix_post": gain(ks[9], D_MODEL),
        "g_ffn_pre": gain(ks[10], D_MODEL),
        "w_gate": nrm(ks[11], (DEPTH, D_MODEL, D_FF), D_MODEL ** -0.5),
        "w_up": nrm(ks[12], (DEPTH, D_MODEL, D_FF), D_MODEL ** -0.5),
        "w_down": nrm(ks[13], (DEPTH, D_FF, D_MODEL), D_FF ** -0.5),
        "g_ffn_post": gain(ks[14], D_MODEL),
    }


def reference(x, g_mix_pre, w_in, b_forget, conv_w, conv_b, conv_ln_g, conv_ln_b,
              w_out, g_mix_post, g_ffn_pre, w_gate, w_up, w_down, g_ffn_post):
    b_, s_, _ = x.shape
    splits = [D_CONV, 2 * D_CONV, 2 * D_CONV + D_ATTN,
              2 * D_CONV + 2 * D_ATTN, 2 * D_CONV + 3 * D_ATTN]

    def heads(t):
        return t.reshape(b_, s_, N_ATTN_HEADS, ATTN_HEAD_DIM).transpose(0, 2, 1, 3)

    for l in range(DEPTH):
        h = rmsnorm(x, g_mix_pre[l])
        proj = h @ w_in[l]
        a_val, a_gate, q, k, v, f_logit = jnp.split(proj, splits, axis=-1)

        u = a_val * jax.nn.sigmoid(a_gate)
        u = causal_depthwise_conv(u, conv_w[l], conv_b[l])
        conv_out = jax.nn.silu(layernorm(u, conv_ln_g[l], conv_ln_b[l]))

        log_f = jax.nn.log_sigmoid(
            f_logit.astype(jnp.float32) + b_forget[l].astype(jnp.float32))
        attn = forgetting_attention(heads(q), heads(k), heads(v),
                                    log_f.transpose(0, 2, 1))
        attn_out = attn.transpose(0, 2, 1, 3).reshape(b_, s_, D_ATTN)

        mix = jnp.concatenate([conv_out, attn_out], axis=-1) @ w_out[l]
        x = x + rmsnorm(mix, g_mix_post[l])

        h = rmsnorm(x, g_ffn_pre[l])
        y = (jax.nn.silu(h @ w_gate[l]) * (h @ w_up[l])) @ w_down[l]
        x = x + rmsnorm(y, g_ffn_post[l])
    return x
```

```python
import numpy as np
from contextlib import ExitStack
import concourse.bass as bass
import concourse.mybir as mybir
from concourse.bass_utils import run_bass_kernel_spmd

F32 = mybir.dt.float32
BF16 = mybir.dt.bfloat16
AF = mybir.ActivationFunctionType
ALU = mybir.AluOpType

D = 1024
S = 8192
NB = 4
DCONV = 512
DATT = 512
NH = 8
CW = 31
DFF = 2816
DIN = 2568
EPS = 1e-6
CH = 512
NCH = S // CH
NOWN = NCH // 2
NBLK = S // 128
NFB = DFF // 128
NFG = NFB // 2
HALO = CW - 1

C_VAL, C_GATE, C_Q, C_K, C_V, C_F = 0, 512, 1024, 1536, 2048, 2560

V_GPRE, V_GPOST, V_GFPRE, V_GFPOST = 0, 8, 16, 24
V_CB, V_LNG, V_LNB = 32, 36, 40
V_CW = 44
V_BF = 168
NV = 200

COMPUTE = ("pe", "act", "dve", "pool")
ALL_ENG = ("pe", "act", "dve", "pool", "sp")


class _Op:
    __slots__ = ("eng", "fn", "is_dma", "semkey", "idx", "w_eng", "w_dma", "flag", "clock", "dma_n")


class Prog:
    def __init__(self, nc, tag):
        self.nc = nc
        self.tag = tag
        self.ops = {e: [] for e in ALL_ENG}
        self.last_w = {}
        self.readers = {}
        self.clock = {e: {} for e in ALL_ENG}
        self.dma_ops = {}
        self.ncomp = {e: 0 for e in ALL_ENG}

    def _record(self, eng, fn, reads, writes, is_dma, semkey):
        op = _Op()
        op.eng, op.fn, op.is_dma, op.semkey = eng, fn, is_dma, semkey
        op.w_eng, op.w_dma, op.flag = {}, {}, False
        op.idx = None if is_dma else self.ncomp[eng]
        clk = self.clock[eng]
        deps = []
        for k in reads:
            w = self.last_w.get(k)
            if w is not None:
                deps.append((w, "raw"))
        for k in writes:
            w = self.last_w.get(k)
            if w is not None:
                deps.append((w, "waw"))
            for r in self.readers.get(k, ()):
                deps.append((r, "war"))
        for d, kind in deps:
            if d is op:
                continue
            if d.is_dma:
                if clk.get(("dma", d.semkey), 0) >= d.dma_n:
                    continue
                op.w_dma[d.semkey] = len(self.dma_ops[d.semkey])
            else:
                if d.eng == eng and (not is_dma):
                    if eng == "pe":
                        continue
                if clk.get(d.eng, 0) >= d.idx + 1:
                    continue
                prev = op.w_eng.get(d.eng)
                if prev is None or prev.idx < d.idx:
                    op.w_eng[d.eng] = d
        for d in op.w_eng.values():
            d.flag = True
            for kk, vv in d.clock.items():
                if clk.get(kk, 0) < vv:
                    clk[kk] = vv
        for sk, n in op.w_dma.items():
            d = self.dma_ops[sk][n - 1]
            for kk, vv in d.clock.items():
                if clk.get(kk, 0) < vv:
                    clk[kk] = vv
        if is_dma:
            lst = self.dma_ops.setdefault(semkey, [])
            lst.append(op)
            op.dma_n = len(lst)
            op.clock = dict(clk)
            op.clock[("dma", semkey)] = op.dma_n
        else:
            self.ncomp[eng] = op.idx + 1
            op.clock = dict(clk)
            op.clock[eng] = op.idx + 1
        for k in reads:
            self.readers.setdefault(k, []).append(op)
        for k in writes:
            self.last_w[k] = op
            self.readers[k] = []
        self.ops[eng].append(op)
        return op

    def op(self, eng, fn, reads=(), writes=()):
        return self._record(eng, fn, tuple(reads), tuple(writes), False, None)

    def dma(self, q, fn, reads=(), writes=(), sem=None):
        return self._record(q, fn, tuple(reads), tuple(writes), True, sem)

    def emit(self):
        nc = self.nc
        esem = {e: nc.alloc_semaphore(name=f"{self.tag}_es_{e}") for e in COMPUTE}
        dsem = {k: nc.alloc_semaphore(name=f"{self.tag}_ds_{i}") for i, k in enumerate(self.dma_ops)}
        val = {}
        for e in COMPUTE:
            c = 0
            for o in self.ops[e]:
                if (not o.is_dma) and o.flag:
                    c += 1
                    val[(e, o.idx)] = c
        getter = {"pe": "tensor", "act": "scalar", "dve": "vector", "pool": "gpsimd", "sp": "sync"}
        with nc.Block() as block:
            for e in ALL_ENG:
                def body(eng, e=e):
                    for o in self.ops[e]:
                        for sk, n in o.w_dma.items():
                            eng.wait_ge(dsem[sk], 16 * n)
                        for de, d in o.w_eng.items():
                            eng.wait_ge(esem[de], val[(de, d.idx)])
                        ins = o.fn(eng)
                        if o.is_dma:
                            ins.then_inc(dsem[o.semkey], 16)
                        elif o.flag:
                            ins.then_inc(esem[e], 1)
                    if e == "sp":
                        for k, lst in self.dma_ops.items():
                            eng.wait_ge(dsem[k], 16 * len(lst))

                getattr(block, getter[e])(body)
        return len(esem) + len(dsem)


class Rot:
    def __init__(self, n):
        self.n, self.i = n, 0

    def next(self):
        r = self.i % self.n
        self.i += 1
        return r


def build_program(debug=False, stop_after=3):
    nc = bass.Bass("TRN2", target_bir_lowering=False)
    dt_in = lambda name, shape: nc.dram_tensor(name, shape, F32, kind="ExternalInput").ap()
    xT_d = dt_in("xT", [NCH, 128, 8, CH])
    kmask_d = dt_in("kmask", [128, NBLK])
    win_d = dt_in("w_in", [128, 8, DIN])
    wout_d = dt_in("w_out", [128, 8, D])
    wg_d = dt_in("w_gate", [NFG, 128, 8, 256])
    wu_d = dt_in("w_up", [NFG, 128, 8, 256])
    wd_d = dt_in("w_down", [8, 128, NFB, 128])
    vecs_d = dt_in("vecs", [128, NV])
    cst_d = dt_in("consts", [3, 128, 128])
    yT_d = nc.dram_tensor("yT", [NOWN, 128, 8, CH], F32, kind="ExternalOutput").ap()
    dk = dict(kind="ExternalOutput") if debug else {}
    kT_s = nc.dram_tensor("kT_s", [NH * 64, S], BF16, **dk).ap()
    qT_s = nc.dram_tensor("qT_s", [NH * 64, NOWN * CH], BF16, **dk).ap()
    crow_s = nc.dram_tensor("crow_s", [NH, NOWN * CH], BF16, **dk).ap()
    attn_s = nc.dram_tensor("attn_s", [DATT, NOWN * CH], BF16, **dk).ap()
    conv_s = nc.dram_tensor("conv_s", [DCONV, NOWN * CH], BF16, **dk).ap()
    v_s = nc.dram_tensor("v_s", [NBLK, 128, DATT], BF16).ap()
    wg_b = nc.dram_tensor("wg_b", [NFG, 128, 8, 256], BF16).ap()
    wu_b = nc.dram_tensor("wu_b", [NFG, 128, 8, 256], BF16).ap()
    wd_b = nc.dram_tensor("wd_b", [8, 128, NFB, 128], BF16).ap()

    sb = lambda name, shape, dt: nc.alloc_sbuf_tensor("s_" + name, shape, dt)
    vecs = sb("vecs", [128, NV], F32)
    ident = sb("ident", [128, 128], F32)
    tri = sb("tri", [128, 128], F32)
    cmask = sb("cmask", [128, 128], BF16)
    ones_bf = sb("ones_bf", [128, 128], BF16)
    ones_f = sb("ones_f", [128, 128], F32)
    epsb = sb("epsb", [128, 1], F32)
    Lall = sb("Lall", [128, NBLK, NH], F32)
    cbias = sb("cbias", [128, NBLK, NH], F32)
    pbig = nc.alloc_psum_tensor("pbig", [128, 8, 512], F32)
    pb = [pbig[:, i, :] for i in range(8)]

    def V(off, n=1):
        return vecs[:, off:off + n]

    with ExitStack() as es:
        P = Prog(nc, "p1")
        st = lambda name, shape, dt: es.enter_context(nc.sbuf_tensor("s_" + name, shape, dt))
        win = st("win", [128, 8, DIN], BF16)
        xs = [st(f"xs{i}", [128, 8, CH], F32) for i in range(2)]
        lsq = st("lsq", [128, 8, CH], BF16)
        wdiag = st("wdiag", [128, 4 * CW, 128], BF16)
        sq = st("sq", [128, 8, CH], BF16)
        hT = [st(f"hT{i}", [128, 8, CH], BF16) for i in range(2)]
        rt = st("rt", [128, CH], F32)
        rstd = st("rstd", [128, CH], F32)
        kst = [st(f"kst{i}", [128, CH], BF16) for i in range(2)]
        qst = [st(f"qst{i}", [128, CH], BF16) for i in range(2)]
        vst = [st(f"vst{i}", [128, CH], BF16) for i in range(2)]
        sg = [st(f"sg{i}", [128, CH], F32) for i in range(2)]
        ubuf = [st(f"ubuf{i}", [128, 4, HALO + CH], BF16) for i in range(2)]
        hsg = st("hsg", [128, 4, 32], F32)
        acc = st("acc", [128, 4, CH], F32)
        cvst = [st(f"cvst{i}", [128, CH], BF16) for i in range(2)]
        mean = st("mean", [128, CH], F32)
        lrt = st("lrt", [128, CH], F32)
        lrs = st("lrs", [128, CH], F32)
        m2 = st("m2", [128, CH], F32)
        tn = [st(f"tn{i}", [128, CH], F32) for i in range(4)]
        zt = st("zt", [128, 32], F32)
        et = st("et", [128, 32], F32)
        kmask = st("kmask_sb", [128, NBLK], F32)
        tot = st("tot", [128, NBLK, NH], F32)
        scA = st("scA", [128, NBLK, NH], F32)
        scB = st("scB", [128, NBLK, NH], F32)
        cst = st("cst", [32, NOWN, 128], BF16)
        ones_row = st("ones_row", [128, 8], F32)

        P.dma("sp", lambda e: e.dma_start(out=vecs[:], in_=vecs_d), writes=["vecs"], sem="c_vecs")
        P.dma("sp", lambda e: e.dma_start(out=ident[:], in_=cst_d[0]), writes=["ident"], sem="c_id")
        P.dma("sp", lambda e: e.dma_start(out=tri[:], in_=cst_d[1]), writes=["tri"], sem="c_tri")
        P.dma("sp", lambda e: e.dma_start(out=kmask[:], in_=kmask_d), writes=["kmask"], sem="c_km")
        P.dma("pool", lambda e: e.dma_start(out=cmask[:], in_=cst_d[2]), writes=["cmask"], sem="c_cm")
        for lo, hi in ((C_K, C_V), (C_V, DIN), (0, C_Q), (C_Q, C_K)):
            P.dma("pool", lambda e, lo=lo, hi=hi: e.dma_start(out=win[:, :, lo:hi], in_=win_d[:, :, lo:hi]),
                  writes=["win"], sem="c_win")
        P.op("pool", lambda e: e.memset(ones_bf[:], 1.0), writes=["ones_bf"])
        P.op("pool", lambda e: e.memset(ones_f[:], 1.0), writes=["ones_f"])
        P.op("pool", lambda e: e.memset(epsb[:], EPS), writes=["epsb"])

        ps = Rot(6)
        pend = None
        readyB = None

        def mm_group(bank, nk, lhs, rhs, out_ap, reads, wkey):
            for k in range(nk):
                P.op("pe", lambda e, k=k: e.matmul(out_ap, lhsT=lhs(k), rhs=rhs(k), start=(k == 0), stop=(k == nk - 1)),
                     reads=reads, writes=[wkey])

        def emit_norm(i):
            sl = i % 2
            X, H = xs[sl], hT[sl]
            xkey = ("xs", sl)
            P.dma("sp" if i == 0 else "pool", lambda e: e.dma_start(out=X[:], in_=xT_d[i]), writes=[xkey], sem=(xkey, i == 0))
            P.op("act", lambda e: e.activation(out=sq[:], in_=X[:], func=AF.Square), reads=[xkey], writes=["sq"])
            mm_group(7, 8, lambda k: ones_bf[:], lambda k: sq[:, k, :], pb[7][:], ["ones_bf", "sq"], ("pb", 7))
            P.op("act", lambda e: e.activation(out=rt[:], in_=pb[7][:], func=AF.Ln, bias=epsb[:, 0:1], scale=1.0 / D),
                 reads=[("pb", 7), "epsb"], writes=["rt"])
            P.op("act", lambda e: e.activation(out=rstd[:], in_=rt[:], func=AF.Exp, scale=-0.5), reads=["rt"], writes=["rstd"])
            for k in range(8):
                P.op("dve", lambda e, k=k: e.scalar_tensor_tensor(
                    out=H[:, k, :], in0=X[:, k, :], scalar=V(V_GPRE + k), in1=rstd[:], op0=ALU.mult, op1=ALU.mult),
                    reads=[xkey, "vecs", "rstd"], writes=[("hT", sl, k)])

        emit_norm(0)
        wd_todo = list(range(4 * CW))

        def wdiag_some(n):
            for _ in range(n):
                if wd_todo:
                    idx = wd_todo.pop(0)
                    P.op("dve", lambda e, idx=idx: e.tensor_scalar(out=wdiag[:, idx, :], in0=ident[:], scalar1=V(V_CW + idx), scalar2=None, op0=ALU.mult),
                         reads=["ident", "vecs"], writes=[("wdiag", idx)])
        for i in range(NCH):
            own = (i % 2 == 1)
            j = i // 2
            sl = i % 2
            X, H = xs[sl], hT[sl]
            xkey = ("xs", sl)
            U = ubuf[j % 2]
            ukh, ukm = ("ubuf_h", j % 2), ("ubuf_m", j % 2)
            hkeys = [("hT", sl, k) for k in range(8)]

            for hp in range(4):
                b = ps.next()
                c0 = C_K + hp * 128
                mm_group(b, 8, lambda k, c0=c0: win[:, k, c0:c0 + 128], lambda k, H=H: H[:, k, :], pb[b][:],
                         ["win"] + hkeys, ("pb", b))
                ks = hp % 2
                P.op("dve", lambda e, b=b, ks=ks: e.tensor_copy(out=kst[ks][:], in_=pb[b][:]),
                     reads=[("pb", b)], writes=[("kst", ks)])
                wdiag_some(6)
                P.dma("sp", lambda e, ks=ks, hp=hp, i=i: e.dma_start(
                    out=kT_s[hp * 128:(hp + 1) * 128, i * CH:(i + 1) * CH], in_=kst[ks][:]),
                    reads=[("kst", ks)], sem=("kst", ks))
            for tb in range(4):
                b = ps.next()
                mm_group(b, 8, lambda k, H=H, tb=tb: H[:, k, tb * 128:(tb + 1) * 128],
                         lambda k: win[:, k, C_V:C_V + 512], pb[b][:], ["win"] + hkeys, ("pb", b))
                blk = i * 4 + tb
                vs = tb % 2
                P.op("dve", lambda e, b=b, vs=vs: e.tensor_copy(out=vst[vs][:], in_=pb[b][:]),
                     reads=[("pb", b)], writes=[("vst", vs)])
                wdiag_some(6)
                P.dma("sp", lambda e, vs=vs, blk=blk: e.dma_start(out=v_s[blk], in_=vst[vs][:]),
                      reads=[("vst", vs)], sem=("vst", vs))
            if i + 1 < NCH:
                emit_norm(i + 1)
            for tb in range(4):
                mm_group(6, 8, lambda k, H=H, tb=tb: H[:, k, tb * 128:(tb + 1) * 128],
                         lambda k: win[:, k, C_F:C_F + 8], pb[6][:, tb * 8:(tb + 1) * 8], ["win"] + hkeys, ("pb", 6))
            P.op("dve", lambda e: e.tensor_tensor(out=zt[:], in0=pb[6][:, 0:32], in1=V(V_BF, 32), op=ALU.add),
                 reads=[("pb", 6), "vecs"], writes=["zt"])
            P.op("act", lambda e: e.activation(out=et[:], in_=zt[:], func=AF.Exp, scale=-1.0), reads=["zt"], writes=["et"])
            P.op("act", lambda e, i=i: e.activation(
                out=Lall[:, i * 4:(i + 1) * 4, :], in_=et[:].rearrange("p (a h) -> p a h", h=NH), func=AF.Ln, bias=1.0, scale=1.0),
                reads=["et"], writes=["Lall"])

            if not own:
                bh = ps.next()
                for cb in range(4):
                    for g in range(2):
                        c0 = (C_VAL if g == 0 else C_GATE) + cb * 128
                        col = (cb * 2 + g) * 32
                        mm_group(bh, 8, lambda k, c0=c0: win[:, k, c0:c0 + 128],
                                 lambda k, H=H: H[:, k, CH - HALO:CH], pb[bh][:, col:col + HALO],
                                 ["win"] + hkeys, ("pb", bh))
                hv = pb[bh][:, 0:256].rearrange("p (c g w) -> p c g w", c=4, g=2)
                P.op("act", lambda e, hv=hv: e.activation(out=hsg[:, :, 0:HALO], in_=hv[:, :, 1, 0:HALO], func=AF.Sigmoid),
                     reads=[("pb", bh)], writes=["hsg"])
                P.op("dve", lambda e, hv=hv, U=U: e.tensor_tensor(out=U[:, :, 0:HALO], in0=hv[:, :, 0, 0:HALO],
                                                                  in1=hsg[:, :, 0:HALO], op=ALU.mult),
                     reads=[("pb", bh), "hsg"], writes=[ukh])
                if pend is not None:
                    pend[0]()
                    readyB = pend[1]
                    pend = None
                continue

            for hp in range(4):
                b = ps.next()
                c0 = C_Q + hp * 128
                mm_group(b, 8, lambda k, c0=c0: win[:, k, c0:c0 + 128], lambda k, H=H: H[:, k, :], pb[b][:],
                         ["win"] + hkeys, ("pb", b))
                qs = hp % 2
                P.op("dve", lambda e, b=b, qs=qs: e.tensor_scalar(out=qst[qs][:], in0=pb[b][:], scalar1=0.125, scalar2=None, op0=ALU.mult),
                     reads=[("pb", b)], writes=[("qst", qs)])
                P.dma("sp", lambda e, qs=qs, hp=hp, j=j: e.dma_start(
                    out=qT_s[hp * 128:(hp + 1) * 128, j * CH:(j + 1) * CH], in_=qst[qs][:]),
                    reads=[("qst", qs)], sem=("qst", qs))
            for cb in range(4):
                bg = ps.next()
                c0 = C_GATE + cb * 128
                mm_group(bg, 8, lambda k, c0=c0: win[:, k, c0:c0 + 128], lambda k, H=H: H[:, k, :], pb[bg][:],
                         ["win"] + hkeys, ("pb", bg))
                s2 = cb % 2
                P.op("act", lambda e, bg=bg, s2=s2: e.activation(out=sg[s2][:], in_=pb[bg][:], func=AF.Sigmoid),
                     reads=[("pb", bg)], writes=[("sg", s2)])
                bv = ps.next()
                c0 = C_VAL + cb * 128
                mm_group(bv, 8, lambda k, c0=c0: win[:, k, c0:c0 + 128], lambda k, H=H: H[:, k, :], pb[bv][:],
                         ["win"] + hkeys, ("pb", bv))
                P.op("dve", lambda e, bv=bv, s2=s2, cb=cb, U=U: e.tensor_tensor(
                    out=U[:, cb, HALO:HALO + CH], in0=pb[bv][:], in1=sg[s2][:], op=ALU.mult),
                    reads=[("pb", bv), ("sg", s2)], writes=[(ukm, cb)])
            if readyB is not None:
                readyB()
                readyB = None

            def make_A(U=U, ukh=ukh, ukm=ukm):
                def stageA():
                    for cb in range(4):
                        b = ps.next()
                        for t in range(CW):
                            P.op("pe", lambda e, cb=cb, t=t, b=b: e.matmul(
                                pb[b][:], lhsT=wdiag[:, cb * CW + t, :], rhs=U[:, cb, t:t + CH], start=(t == 0), stop=(t == CW - 1)),
                                reads=[("wdiag", cb * CW + t), ukh, (ukm, cb)], writes=[("pb", b)])
                        P.op("dve", lambda e, cb=cb, b=b: e.tensor_scalar(
                            out=acc[:, cb, :], in0=pb[b][:], scalar1=V(V_CB + cb), scalar2=None, op0=ALU.add),
                            reads=[("pb", b), "vecs"], writes=[("acc", cb)])
                    acck = [("acc", cb) for cb in range(4)]
                    P.op("act", lambda e: e.activation(out=lsq[:, 0:4, :], in_=acc[:], func=AF.Copy), reads=acck, writes=["lsq"])
                    P.op("act", lambda e: e.activation(out=lsq[:, 4:8, :], in_=acc[:], func=AF.Square), reads=acck, writes=["lsq"])
                return stageA

            def make_B(j=j):
                def stageB():
                    b1 = ps.next()
                    mm_group(b1, 4, lambda k: ones_bf[:], lambda k: lsq[:, k, :], pb[b1][:], ["ones_bf", "lsq"], ("pb", b1))
                    b2 = ps.next()
                    mm_group(b2, 4, lambda k: ones_bf[:], lambda k: lsq[:, 4 + k, :], pb[b2][:], ["ones_bf", "lsq"], ("pb", b2))
                    P.op("act", lambda e: e.activation(out=mean[:], in_=pb[b1][:], func=AF.Copy, scale=1.0 / DCONV),
                         reads=[("pb", b1)], writes=["mean"])
                    P.op("dve", lambda e: e.tensor_tensor(out=m2[:], in0=mean[:], in1=mean[:], op=ALU.mult), reads=["mean"], writes=["m2"])
                    P.op("dve", lambda e: e.scalar_tensor_tensor(out=m2[:], in0=pb[b2][:], scalar=1.0 / DCONV, in1=m2[:],
                                                                 op0=ALU.mult, op1=ALU.subtract),
                         reads=[("pb", b2), "m2"], writes=["m2"])
                    P.op("act", lambda e: e.activation(out=lrt[:], in_=m2[:], func=AF.Ln, bias=epsb[:, 0:1], scale=1.0),
                         reads=["m2", "epsb"], writes=["lrt"])
                    P.op("act", lambda e: e.activation(out=lrs[:], in_=lrt[:], func=AF.Exp, scale=-0.5), reads=["lrt"], writes=["lrs"])
                    for cb in range(4):
                        P.op("dve", lambda e, cb=cb: e.tensor_tensor(out=tn[cb][:], in0=acc[:, cb, :], in1=mean[:], op=ALU.subtract),
                             reads=[("acc", cb), "mean"], writes=[("tn", cb)])
                    for cb in range(4):
                        P.op("dve", lambda e, cb=cb: e.tensor_tensor(out=tn[cb][:], in0=tn[cb][:], in1=lrs[:], op=ALU.mult),
                             reads=[("tn", cb), "lrs"], writes=[("tn", cb)])
                    for cb in range(4):
                        t2 = cb % 2
                        P.op("act", lambda e, cb=cb, t2=t2: e.activation(
                            out=cvst[t2][:], in_=tn[cb][:], func=AF.Silu, bias=V(V_LNB + cb), scale=V(V_LNG + cb)),
                            reads=[("tn", cb), "vecs"], writes=[("cvst", t2)])
                        P.dma("sp", lambda e, cb=cb, t2=t2: e.dma_start(
                            out=conv_s[cb * 128:(cb + 1) * 128, j * CH:(j + 1) * CH], in_=cvst[t2][:]),
                            reads=[("cvst", t2)], sem=("cvst", t2))
                return stageB

            assert pend is None
            pend = (make_A(), make_B())

        L2 = Lall[:].rearrange("p b h -> p (b h)")
        P.op("pe", lambda e: e.matmul(pb[0][:], lhsT=tri[:], rhs=L2, start=True, stop=True), reads=["tri", "Lall"], writes=[("pb", 0)])
        P.op("pe", lambda e: e.matmul(pb[1][:], lhsT=ones_f[:], rhs=L2, start=True, stop=True), reads=["ones_f", "Lall"], writes=[("pb", 1)])
        P.op("dve", lambda e: e.tensor_copy(out=tot[:].rearrange("p b h -> p (b h)"), in_=pb[1][:]), reads=[("pb", 1)], writes=["tot"])
        P.op("act", lambda e: e.activation(out=scA[:].rearrange("p b h -> p (b h)"), in_=pb[1][:], func=AF.Copy),
             reads=[("pb", 1)], writes=["scA"])
        cur, oth, ck, ok = scA, scB, "scA", "scB"
        s = 1
        while s < NBLK:
            P.op("dve", lambda e, cur=cur, oth=oth, s=s: e.tensor_copy(out=oth[:, 0:s, :], in_=cur[:, 0:s, :]), reads=[ck], writes=[ok])
            P.op("dve", lambda e, cur=cur, oth=oth, s=s: e.tensor_tensor(out=oth[:, s:NBLK, :], in0=cur[:, s:NBLK, :],
                                                                       in1=cur[:, 0:NBLK - s, :], op=ALU.add),
                 reads=[ck], writes=[ok])
            cur, oth, ck, ok = oth, cur, ok, ck
            s *= 2
        P.op("dve", lambda e, cur=cur: e.tensor_tensor(out=tot[:], in0=cur[:], in1=tot[:], op=ALU.subtract), reads=[ck, "tot"], writes=["tot"])
        P.op("dve", lambda e: e.tensor_tensor(out=tot[:].rearrange("p b h -> p (b h)"), in0=tot[:].rearrange("p b h -> p (b h)"),
                                              in1=pb[0][:], op=ALU.add), reads=["tot", ("pb", 0)], writes=["tot"])
        for h in range(NH):
            P.op("dve", lambda e, h=h: e.tensor_tensor(out=cbias[:, :, h], in0=tot[:, :, h], in1=kmask[:], op=ALU.add),
                 reads=["tot", "kmask"], writes=["cbias"])
        if pend is not None:
            pend[0]()
            pend[1]()
            pend = None

        for j in range(NOWN):
            blk0 = (2 * j + 1) * 4
            b = 2 + (j // 4)
            P.op("pe", lambda e, b=b, blk0=blk0, j=j: e.matmul(
                pb[b][0:32, (j % 4) * 128:(j % 4 + 1) * 128], lhsT=tot[:, blk0:blk0 + 4, :].rearrange("p b h -> p (b h)"),
                rhs=ident[:], start=True, stop=True),
                reads=["tot", "ident"], writes=[("pb", b)])
        for half in range(2):
            P.op("act", lambda e, half=half: e.activation(
                out=cst[:, half * 4:(half + 1) * 4, :], in_=pb[2 + half][0:32, :].rearrange("p (j t) -> p j t", j=4),
                func=AF.Copy, scale=-1.0),
                reads=[("pb", 2 + half)], writes=["cst"])
        crow_v = crow_s.rearrange("h (j bl t) -> h j bl t", j=NOWN, bl=4)
        for bl in range(4):
            P.dma("sp", lambda e, bl=bl: e.dma_start(out=crow_v[:, :, bl, :], in_=cst[bl * 8:(bl + 1) * 8, :, :]),
                  reads=["cst"], sem="cst")
        P.emit()
    if stop_after == 1:
        return nc

    wout = sb("wout", [128, 8, D], BF16)
    with ExitStack() as es:
        P = Prog(nc, "p2")
        st = lambda name, shape, dt: es.enter_context(nc.sbuf_tensor("s_" + name, shape, dt))
        kT = [st(f"kT{i}", [65, S], BF16) for i in range(2)]
        qT = [st(f"qT{i}", [65, NOWN * CH], BF16) for i in range(2)]
        Vh = [st(f"Vh{i}", [128, NBLK, 128], BF16) for i in range(2)]
        NPT = 4
        pt = [st(f"pt{i}", [128, 2, CH], BF16) for i in range(NPT)]
        osb = [st(f"osb{i}", [65, CH], F32) for i in range(2)]
        rec = [st(f"rec{i}", [65, CH], F32) for i in range(2)]
        ost = [st(f"ost{i}", [64, CH], BF16) for i in range(2)]
        rhi = [st(f"rhi{i}", [65, CH], BF16) for i in range(2)]
        rlo = [st(f"rlo{i}", [65, CH], BF16) for i in range(2)]
        fill = st("fill", [128, 256], BF16)
        P.op("pool", lambda e: e.memset(fill[:], 0.5), writes=["fill"])
        def init_slot(i):
            for g4 in range(4):
                P.op("dve", lambda e, g4=g4: e.memset(kT[i][64:65, g4 * 2048:(g4 + 1) * 2048], 1.0), writes=[("kT1", i, g4)])
            P.op("pool", lambda e: e.memset(Vh[i][:, :, 64:128], 0.0), writes=[("Vh1", i)])
            P.op("pool", lambda e: e.memset(Vh[i][:, :, 64:65], 1.0), writes=[("Vh1", i)])

        init_slot(0)
        cvt = [lambda e, fg=fg: e.dma_start(out=wg_b[fg], in_=wg_d[fg]) for fg in range(NFG)]
        cvt += [lambda e, fg=fg: e.dma_start(out=wu_b[fg], in_=wu_d[fg]) for fg in range(NFG)]
        cvt += [lambda e, ob=ob: e.dma_start(out=wd_b[ob], in_=wd_d[ob]) for ob in range(8)]
        cvt = [lambda e: e.dma_start(out=wout[:], in_=wout_d)] + cvt
        sbank = Rot(4)
        ptr = Rot(NPT)
        it = 0
        def load_head(h):
            sl = h % 2
            P.dma("sp", lambda e: e.dma_start(out=qT[sl][0:64, :], in_=qT_s[h * 64:(h + 1) * 64, :]),
                  writes=[("qT", sl)], sem=("qT", sl))
            P.dma("sp", lambda e: e.dma_start(out=qT[sl][64:65, :], in_=crow_s[h:h + 1, :]),
                  writes=[("qT", sl)], sem=("qT", sl))
            for g4 in range(4):
                P.dma("sp", lambda e, g4=g4: e.dma_start(
                    out=kT[sl][0:64, g4 * 2048:(g4 + 1) * 2048], in_=kT_s[h * 64:(h + 1) * 64, g4 * 2048:(g4 + 1) * 2048]),
                    writes=[("kT", sl, g4)], sem=("kT", sl, g4))
                P.dma("sp", lambda e, g4=g4: e.dma_start(
                    out=Vh[sl][:, g4 * 16:(g4 + 1) * 16, 0:64],
                    in_=v_s[g4 * 16:(g4 + 1) * 16, :, h * 64:(h + 1) * 64].rearrange("b p d -> p b d")),
                    writes=[("Vh", sl, g4)], sem=("Vh", sl, g4))

        pvq = []
        normB_cd = []
        half = [0]

        def pop_pv():
            t = pvq.pop(0)
            for c in range(t["n"]):
                po, qlo, first, last, norm = t["po"][c], t["qlo"], t["first"], t["last"], t["norm"][c]
                P.op("pe", lambda e, t=t, c=c, po=po, qlo=qlo, first=first, last=last: e.matmul(
                    pb[po][:, qlo:CH], lhsT=Vh[t["sl"]][:, t["kb"], :], rhs=pt[t["pi"]][:, c, qlo:CH], start=first, stop=last),
                    reads=[("Vh", t["sl"], t["kb"] // 16), ("Vh1", t["sl"]), ("pt", t["pi"])], writes=[("pb", po)])
                if last:
                    norm[0]()
                    normB_cd.append([8, norm[1]])

        def make_norm(po, o2, h, j):
            def normA():
                P.op("dve", lambda e: e.tensor_copy(out=osb[o2][:], in_=pb[po][0:65, :]),
                     reads=[("pb", po)], writes=[("osb", o2)])
                P.op("dve", lambda e: e.reciprocal(out=rec[o2][64:65, :], in_=osb[o2][64:65, :]),
                     reads=[("osb", o2)], writes=[("rec", o2)])
                P.op("dve", lambda e: e.tensor_copy(out=rhi[o2][64:65, :], in_=rec[o2][64:65, :]),
                     reads=[("rec", o2)], writes=[("rhi", o2)])
                P.op("dve", lambda e: e.tensor_tensor(out=rlo[o2][64:65, :], in0=rec[o2][64:65, :], in1=rhi[o2][64:65, :], op=ALU.subtract),
                     reads=[("rec", o2), ("rhi", o2)], writes=[("rlo", o2)])

            def normB():
                P.op("pe", lambda e: e.matmul(pb[6][0:64, :], lhsT=ones_bf[64:65, 0:64], rhs=rhi[o2][64:65, :],
                                              start=True, stop=False),
                     reads=["ones_bf", ("rhi", o2)], writes=[("pb", 6)])
                P.op("pe", lambda e: e.matmul(pb[6][0:64, :], lhsT=ones_bf[64:65, 0:64], rhs=rlo[o2][64:65, :],
                                              start=False, stop=True),
                     reads=["ones_bf", ("rlo", o2)], writes=[("pb", 6)])
                P.op("dve", lambda e: e.tensor_tensor(out=ost[o2][:], in0=osb[o2][0:64, :], in1=pb[6][0:64, :], op=ALU.mult),
                     reads=[("osb", o2), ("pb", 6)], writes=[("ost", o2)])
                P.dma("sp", lambda e: e.dma_start(
                    out=attn_s[h * 64:(h + 1) * 64, j * CH:(j + 1) * CH], in_=ost[o2][:]),
                    reads=[("ost", o2)], sem=("ost", o2))
            return normA, normB

        def emit_tile(h, sl, kb, chunks, dk_, first, last, norms):
            n = len(chunks)
            qlo = 128 * dk_ if dk_ > 0 else 0
            if n == 2 and half[0] % 2 == 1:
                half[0] += 1
            hb = half[0] % 4
            half[0] += n
            pi = ptr.next()
            for c, j in enumerate(chunks):
                P.op("pe", lambda e, c=c, j=j: e.matmul(
                    pb[hb + c][:, qlo:CH], lhsT=kT[sl][:, kb * 128:(kb + 1) * 128],
                    rhs=qT[sl][:, j * CH + qlo:(j + 1) * CH], start=True, stop=True),
                    reads=[("kT", sl, kb // 16), ("kT1", sl, kb // 16), ("qT", sl)], writes=[("pb", hb + c)])
            for _ in range(1 if n == 2 else 2):
                P.op("pe", lambda e: e.matmul(pb[7][:, 0:256], lhsT=ones_bf[:], rhs=fill[:], start=True, stop=True),
                     reads=["fill"], writes=[("pb", 7)])
            if n == 2:
                P.op("act", lambda e: e.activation(
                    out=pt[pi][:], in_=pbig[:, hb:hb + 2, :], func=AF.Exp, bias=cbias[:, kb, h:h + 1], scale=1.0),
                    reads=[("pb", hb), ("pb", hb + 1), "cbias"], writes=[("pt", pi)])
            else:
                P.op("act", lambda e: e.activation(
                    out=pt[pi][:, 0, qlo:CH], in_=pb[hb][:, qlo:CH], func=AF.Exp, bias=cbias[:, kb, h:h + 1], scale=1.0),
                    reads=[("pb", hb), "cbias"], writes=[("pt", pi)])
            if dk_ >= 0:
                assert n == 1
                P.op("pool", lambda e: e.tensor_tensor(
                    out=pt[pi][:, 0, 128 * dk_:128 * dk_ + 128], in0=pt[pi][:, 0, 128 * dk_:128 * dk_ + 128], in1=cmask[:], op=ALU.mult),
                    reads=[("pt", pi), "cmask"], writes=[("pt", pi)])
            pvq.append(dict(n=n, sl=sl, kb=kb, pi=pi, qlo=qlo, first=first, last=last,
                            po=[4 + (j % 2) for j in chunks], norm=norms))
            if normB_cd:
                normB_cd[0][0] -= 1
                if normB_cd[0][0] <= 0:
                    normB_cd.pop(0)[1]()
            if len(pvq) > 2:
                pop_pv()

        load_head(0)
        for h in range(NH):
            sl = h % 2
            if h + 1 < NH:
                while pvq:
                    pop_pv()
                load_head(h + 1)
            if h >= 1:
                for _ in range(6):
                    if cvt:
                        P.dma("pool", cvt.pop(0), sem="cvt")
            for pr in range(NOWN // 2):
                if h == 0 and pr == 1:
                    init_slot(1)
                j0, j1 = 2 * pr, 2 * pr + 1
                da, db = 8 * j0 + 4, 8 * j1 + 4
                n0 = make_norm(4, 0, h, j0)
                n1 = make_norm(5, 1, h, j1)
                emit_tile(h, sl, 0, [j0, j1], -1, True, False, [n0, n1])
                for dk_ in (3, 2, 1):
                    emit_tile(h, sl, da + dk_, [j0], dk_, False, False, [n0])
                for kb in range(1, da):
                    emit_tile(h, sl, kb, [j0, j1], -1, False, False, [n0, n1])
                emit_tile(h, sl, da, [j0], 0, False, True, [n0])
                for dk_ in (3, 2, 1):
                    emit_tile(h, sl, db + dk_, [j1], dk_, False, False, [n1])
                for kb in range(da, db):
                    emit_tile(h, sl, kb, [j1], -1, False, False, [n1])
                emit_tile(h, sl, db, [j1], 0, False, True, [n1])
        while pvq:
            pop_pv()
        while normB_cd:
            normB_cd.pop(0)[1]()
        assert not cvt
        P.emit()
    if stop_after == 2:
        return nc

    with ExitStack() as es:
        P = Prog(nc, "p3")
        st = lambda name, shape, dt: es.enter_context(nc.sbuf_tensor("s_" + name, shape, dt))
        xs = [st(f"x3_{i}", [128, 8, CH], F32) for i in range(2)]
        at = [st(f"at{i}", [128, 4, CH], BF16) for i in range(2)]
        cv = [st(f"cv{i}", [128, 4, CH], BF16) for i in range(2)]
        mix = st("mix", [128, 8, CH], F32)
        yb = st("yb", [128, 8, CH], F32)
        sq = st("sq3", [128, 8, CH], BF16)
        h2 = [st(f"h2_{i}", [128, 8, CH], BF16) for i in range(2)]
        act = st("actT", [128, NFB, CH], BF16)
        rt = st("rt3", [128, CH], F32)
        rstd = st("rstd3", [128, CH], F32)
        tmp = [st(f"tmp{i}", [128, CH], F32) for i in range(4)]
        sgt = [st(f"sgt{i}", [128, CH], F32) for i in range(2)]
        NWB = 3
        wg = [st(f"wg{i}", [128, 8, 256], BF16) for i in range(NWB)]
        wu = [st(f"wu{i}", [128, 8, 256], BF16) for i in range(NWB)]
        wd = [st(f"wd{i}", [128, NFB, 128], BF16) for i in range(NWB)]
        ps = Rot(6)
        wgr, wdr = Rot(NWB), Rot(NWB)
        xkeys = lambda sl: [("x3", sl, k) for k in range(8)]
        mk = [("mix", ob) for ob in range(8)]
        yk = [("yb", ob) for ob in range(8)]

        def load_chunk(j):
            sl = j % 2
            P.dma("sp", lambda e: e.dma_start(
                out=at[sl][:], in_=attn_s[:, j * CH:(j + 1) * CH].rearrange("(c p) t -> p c t", p=128)),
                writes=[("at", sl)], sem=("at", sl))
            P.dma("sp", lambda e: e.dma_start(
                out=cv[sl][:], in_=conv_s[:, j * CH:(j + 1) * CH].rearrange("(c p) t -> p c t", p=128)),
                writes=[("cv", sl)], sem=("cv", sl))
            P.dma("pool", lambda e: e.dma_start(out=xs[sl][:], in_=xT_d[2 * j + 1]), writes=xkeys(sl), sem=("x3", sl))

        def square(src, srckeys):
            P.op("act", lambda e: e.activation(out=sq[:], in_=src[:], func=AF.Square), reads=srckeys, writes=["sq"])

        def stats():
            for k in range(8):
                P.op("pe", lambda e, k=k: e.matmul(pb[7][:], lhsT=ones_bf[:], rhs=sq[:, k, :], start=(k == 0), stop=(k == 7)),
                     reads=["ones_bf", "sq"], writes=[("pb", 7)])
            P.op("act", lambda e: e.activation(out=rt[:], in_=pb[7][:], func=AF.Ln, bias=epsb[:, 0:1], scale=1.0 / D),
                 reads=[("pb", 7), "epsb"], writes=["rt"])
            P.op("act", lambda e: e.activation(out=rstd[:], in_=rt[:], func=AF.Exp, scale=-0.5), reads=["rt"], writes=["rstd"])

        def resid_apply(src, srckey, goff, sl):
            X = xs[sl]
            for g in range(2):
                for ob in range(g * 4, g * 4 + 4):
                    t2 = ob % 4
                    P.op("dve", lambda e, ob=ob, t2=t2: e.tensor_tensor(out=tmp[t2][:], in0=src[:, ob, :], in1=rstd[:], op=ALU.mult),
                         reads=[(srckey, ob), "rstd"], writes=[("tmp", t2)])
                for ob in range(g * 4, g * 4 + 4):
                    t2 = ob % 4
                    P.op("dve", lambda e, ob=ob, t2=t2: e.scalar_tensor_tensor(
                        out=X[:, ob, :], in0=tmp[t2][:], scalar=V(goff + ob), in1=X[:, ob, :], op0=ALU.mult, op1=ALU.add),
                        reads=[("tmp", t2), ("x3", sl, ob), "vecs"], writes=[("x3", sl, ob)])

        def A1(j):
            sl = j % 2
            A, Cv = at[sl], cv[sl]
            for ob in range(8):
                b = ps.next()
                for k in range(8):
                    rhs = (lambda k=k: Cv[:, k, :]) if k < 4 else (lambda k=k: A[:, k - 4, :])
                    P.op("pe", lambda e, k=k, b=b, ob=ob, rhs=rhs: e.matmul(
                        pb[b][:], lhsT=wout[:, k, ob * 128:(ob + 1) * 128], rhs=rhs(), start=(k == 0), stop=(k == 7)),
                        reads=["wout", ("at", sl), ("cv", sl)], writes=[("pb", b)])
                P.op("act", lambda e, b=b, ob=ob: e.activation(out=mix[:, ob, :], in_=pb[b][:], func=AF.Copy),
                     reads=[("pb", b)], writes=[("mix", ob)])
            square(mix, mk)

        def A2(j):
            sl = j % 2
            stats()
            resid_apply(mix, "mix", V_GPOST, sl)

        def A2b(j):
            sl = j % 2
            square(xs[sl], xkeys(sl))

        def A3(j):
            sl = j % 2
            X, H2 = xs[sl], h2[sl]
            stats()
            for k in range(8):
                P.op("dve", lambda e, k=k: e.scalar_tensor_tensor(
                    out=H2[:, k, :], in0=X[:, k, :], scalar=V(V_GFPRE + k), in1=rstd[:], op0=ALU.mult, op1=ALU.mult),
                    reads=[("x3", sl, k), "rstd", "vecs"], writes=[("h2", sl, k)])

        def up_group(j, fg):
            sl = j % 2
            H2 = h2[sl]
            hk = [("h2", sl, k) for k in range(8)]
            ws = wgr.next()
            P.dma("sp", lambda e: e.dma_start(out=wg[ws][:], in_=wg_b[fg]), writes=[("wg", ws)], sem=("wg", ws))
            P.dma("sp", lambda e: e.dma_start(out=wu[ws][:], in_=wu_b[fg]), writes=[("wu", ws)], sem=("wu", ws))
            for fb2 in range(2):
                fb = fg * 2 + fb2
                bg = ps.next()
                for k in range(8):
                    P.op("pe", lambda e, k=k, bg=bg, fb2=fb2: e.matmul(
                        pb[bg][:], lhsT=wg[ws][:, k, fb2 * 128:(fb2 + 1) * 128], rhs=H2[:, k, :], start=(k == 0), stop=(k == 7)),
                        reads=[("wg", ws)] + hk, writes=[("pb", bg)])
                s2 = fb % 2
                P.op("act", lambda e, bg=bg, s2=s2: e.activation(out=sgt[s2][:], in_=pb[bg][:], func=AF.Silu),
                     reads=[("pb", bg)], writes=[("sgt", s2)])
                bu = ps.next()
                for k in range(8):
                    P.op("pe", lambda e, k=k, bu=bu, fb2=fb2: e.matmul(
                        pb[bu][:], lhsT=wu[ws][:, k, fb2 * 128:(fb2 + 1) * 128], rhs=H2[:, k, :], start=(k == 0), stop=(k == 7)),
                        reads=[("wu", ws)] + hk, writes=[("pb", bu)])
                P.op("dve", lambda e, bu=bu, s2=s2, fb=fb: e.tensor_tensor(out=act[:, fb, :], in0=pb[bu][:], in1=sgt[s2][:], op=ALU.mult),
                     reads=[("pb", bu), ("sgt", s2)], writes=[("act", fb)])

        def down(j):
            ak = [("act", fb) for fb in range(NFB)]
            for ob in range(8):
                ws = wdr.next()
                P.dma("sp", lambda e, ws=ws, ob=ob: e.dma_start(out=wd[ws][:], in_=wd_b[ob]), writes=[("wd", ws)], sem=("wd", ws))
                b = ps.next()
                for fb in range(NFB):
                    P.op("pe", lambda e, fb=fb, b=b, ws=ws: e.matmul(
                        pb[b][:], lhsT=wd[ws][:, fb, :], rhs=act[:, fb, :], start=(fb == 0), stop=(fb == NFB - 1)),
                        reads=[("wd", ws)] + ak, writes=[("pb", b)])
                P.op("act", lambda e, b=b, ob=ob: e.activation(out=yb[:, ob, :], in_=pb[b][:], func=AF.Copy),
                     reads=[("pb", b)], writes=[("yb", ob)])
            square(yb, yk)

        def N2(j):
            sl = j % 2
            stats()
            resid_apply(yb, "yb", V_GFPOST, sl)
            P.dma("pool", lambda e: e.dma_start(out=yT_d[j], in_=xs[sl][:]), reads=xkeys(sl), sem=("x3", sl))

        load_chunk(0)
        A1(0)
        A2(0)
        A2b(0)
        A3(0)
        for j in range(NOWN):
            for fg in range(NFG):
                up_group(j, fg)
                if fg == 0:
                    if j > 0:
                        N2(j - 1)
                    if j + 1 < NOWN:
                        load_chunk(j + 1)
                if j + 1 < NOWN:
                    if fg == 2:
                        A1(j + 1)
                    elif fg == 4:
                        A2(j + 1)
                    elif fg == 7:
                        A2b(j + 1)
                    elif fg == 9:
                        A3(j + 1)
            down(j)
        N2(NOWN - 1)
        P.emit()
    return nc


_NC_CACHE = {}


def _host_inputs(x, g_mix_pre, w_in, b_forget, conv_w, conv_b, conv_ln_g, conv_ln_b,
                 w_out, g_mix_post, g_ffn_pre, w_gate, w_up, w_down, g_ffn_post):
    f32 = np.float32
    x = np.asarray(x, f32)
    pk = lambda v, n: np.asarray(v, f32).reshape(n, 128).T
    vecs = np.zeros((128, NV), f32)
    vecs[:, V_GPRE:V_GPRE + 8] = pk(g_mix_pre[0], 8)
    vecs[:, V_GPOST:V_GPOST + 8] = pk(g_mix_post[0], 8)
    vecs[:, V_GFPRE:V_GFPRE + 8] = pk(g_ffn_pre[0], 8)
    vecs[:, V_GFPOST:V_GFPOST + 8] = pk(g_ffn_post[0], 8)
    vecs[:, V_CB:V_CB + 4] = pk(conv_b[0], 4)
    vecs[:, V_LNG:V_LNG + 4] = pk(conv_ln_g[0], 4)
    vecs[:, V_LNB:V_LNB + 4] = pk(conv_ln_b[0], 4)
    cw = np.asarray(conv_w[0], f32)
    vecs[:, V_CW:V_CW + 4 * CW] = cw.reshape(CW, 4, 128).transpose(2, 1, 0).reshape(128, 4 * CW)
    vecs[:, V_BF:V_BF + 32] = np.tile(np.asarray(b_forget[0], f32), 4)[None, :]
    ar = np.arange(128)
    consts = np.stack([np.eye(128, dtype=f32),
                       (ar[:, None] <= ar[None, :]).astype(f32),
                       (ar[:, None] <= ar[None, :]).astype(f32)])
    lay_kp = lambda w: np.ascontiguousarray(np.asarray(w, f32).reshape(8, 128, -1).transpose(1, 0, 2))
    win_h = lay_kp(w_in[0])
    wout_h = lay_kp(w_out[0])
    grp = lambda w: np.ascontiguousarray(lay_kp(w).reshape(128, 8, NFG, 256).transpose(2, 0, 1, 3))
    wg_h, wu_h = grp(w_gate[0]), grp(w_up[0])
    wd_h = np.ascontiguousarray(np.asarray(w_down[0], f32).reshape(NFB, 128, 8, 128).transpose(2, 1, 0, 3))
    shared = {"w_in": win_h, "w_out": wout_h, "w_gate": wg_h, "w_up": wu_h, "w_down": wd_h,
              "vecs": vecs, "consts": consts}
    in_maps = []
    for c in range(8):
        b, p = c // 2, c % 2
        xb = x[b].reshape(NCH, CH, 8, 128).transpose(0, 3, 2, 1)
        xT = np.zeros((NCH, 128, 8, CH), f32)
        km = np.zeros((128, NBLK), f32)
        if p == 1:
            xT[:] = xb
        else:
            xT[1:] = xb[:NCH - 1]
            km[:, 0:4] = -30000.0
        m = dict(shared)
        m["xT"] = xT
        m["kmask"] = km
        in_maps.append(m)
    return in_maps


def kernel(**inputs):
    if "nc" not in _NC_CACHE:
        _NC_CACHE["nc"] = build_program()
    nc = _NC_CACHE["nc"]
    in_maps = _host_inputs(**inputs)
    res = run_bass_kernel_spmd(nc, in_maps, core_ids=list(range(8)))
    out = np.zeros((NB, S, D), np.float32)
    for c in range(8):
        b, p = c // 2, c % 2
        yT = np.asarray(res.results[c]["yT"], np.float32)
        y = yT.transpose(0, 3, 2, 1).reshape(NOWN, CH, D)
        ov = out[b].reshape(NCH, CH, D)
        ov[p::2] = y
    return out
```

```python
import numpy as np
from contextlib import ExitStack
import concourse.bass as bass
import concourse.mybir as mybir
from concourse.bass_utils import run_bass_kernel_spmd

F32 = mybir.dt.float32
BF16 = mybir.dt.bfloat16
AF = mybir.ActivationFunctionType
ALU = mybir.AluOpType

D = 1024
S = 8192
NB = 4
DCONV = 512
DATT = 512
NH = 8
CW = 31
DFF = 2816
DIN = 2568
EPS = 1e-6
CH = 512
NCH = S // CH
NOWN = NCH // 2
NBLK = S // 128
NFB = DFF // 128
NFG = NFB // 2
HALO = CW - 1

C_VAL, C_GATE, C_Q, C_K, C_V, C_F = 0, 512, 1024, 1536, 2048, 2560

V_GPRE, V_GPOST, V_GFPRE, V_GFPOST = 0, 8, 16, 24
V_CB, V_LNG, V_LNB = 32, 36, 40
V_CW = 44
V_BF = 168
NV = 200

COMPUTE = ("pe", "act", "dve", "pool")
ALL_ENG = ("pe", "act", "dve", "pool", "sp")


class _Op:
    __slots__ = ("eng", "fn", "is_dma", "semkey", "idx", "w_eng", "w_dma", "flag", "clock", "dma_n")


class Prog:
    def __init__(self, nc, tag):
        self.nc = nc
        self.tag = tag
        self.ops = {e: [] for e in ALL_ENG}
        self.last_w = {}
        self.readers = {}
        self.clock = {e: {} for e in ALL_ENG}
        self.dma_ops = {}
        self.ncomp = {e: 0 for e in ALL_ENG}

    def _record(self, eng, fn, reads, writes, is_dma, semkey):
        op = _Op()
        op.eng, op.fn, op.is_dma, op.semkey = eng, fn, is_dma, semkey
        op.w_eng, op.w_dma, op.flag = {}, {}, False
        op.idx = None if is_dma else self.ncomp[eng]
        clk = self.clock[eng]
        deps = []
        for k in reads:
            w = self.last_w.get(k)
            if w is not None:
                deps.append((w, "raw"))
        for k in writes:
            w = self.last_w.get(k)
            if w is not None:
                deps.append((w, "waw"))
            for r in self.readers.get(k, ()):
                deps.append((r, "war"))
        for d, kind in deps:
            if d is op:
                continue
            if d.is_dma:
                if clk.get(("dma", d.semkey), 0) >= d.dma_n:
                    continue
                op.w_dma[d.semkey] = len(self.dma_ops[d.semkey])
            else:
                if d.eng == eng and (not is_dma):
                    if eng == "pe":
                        continue
                if clk.get(d.eng, 0) >= d.idx + 1:
                    continue
                prev = op.w_eng.get(d.eng)
                if prev is None or prev.idx < d.idx:
                    op.w_eng[d.eng] = d
        for d in op.w_eng.values():
            d.flag = True
            for kk, vv in d.clock.items():
                if clk.get(kk, 0) < vv:
                    clk[kk] = vv
        for sk, n in op.w_dma.items():
            d = self.dma_ops[sk][n - 1]
            for kk, vv in d.clock.items():
                if clk.get(kk, 0) < vv:
                    clk[kk] = vv
        if is_dma:
            lst = self.dma_ops.setdefault(semkey, [])
            lst.append(op)
            op.dma_n = len(lst)
            op.clock = dict(clk)
            op.clock[("dma", semkey)] = op.dma_n
        else:
            self.ncomp[eng] = op.idx + 1
            op.clock = dict(clk)
            op.clock[eng] = op.idx + 1
        for k in reads:
            self.readers.setdefault(k, []).append(op)
        for k in writes:
            self.last_w[k] = op
            self.readers[k] = []
        self.ops[eng].append(op)
        return op

    def op(self, eng, fn, reads=(), writes=()):
        return self._record(eng, fn, tuple(reads), tuple(writes), False, None)

    def dma(self, q, fn, reads=(), writes=(), sem=None):
        return self._record(q, fn, tuple(reads), tuple(writes), True, sem)

    def emit(self):
        nc = self.nc
        esem = {e: nc.alloc_semaphore(name=f"{self.tag}_es_{e}") for e in COMPUTE}
        dsem = {k: nc.alloc_semaphore(name=f"{self.tag}_ds_{i}") for i, k in enumerate(self.dma_ops)}
        val = {}
        for e in COMPUTE:
            c = 0
            for o in self.ops[e]:
                if (not o.is_dma) and o.flag:
                    c += 1
                    val[(e, o.idx)] = c
        getter = {"pe": "tensor", "act": "scalar", "dve": "vector", "pool": "gpsimd", "sp": "sync"}
        with nc.Block() as block:
            for e in ALL_ENG:
                def body(eng, e=e):
                    for o in self.ops[e]:
                        for sk, n in o.w_dma.items():
                            eng.wait_ge(dsem[sk], 16 * n)
                        for de, d in o.w_eng.items():
                            eng.wait_ge(esem[de], val[(de, d.idx)])
                        ins = o.fn(eng)
                        if o.is_dma:
                            ins.then_inc(dsem[o.semkey], 16)
                        elif o.flag:
                            ins.then_inc(esem[e], 1)
                    if e == "sp":
                        for k, lst in self.dma_ops.items():
                            eng.wait_ge(dsem[k], 16 * len(lst))

                getattr(block, getter[e])(body)
        return len(esem) + len(dsem)


class Rot:
    def __init__(self, n):
        self.n, self.i = n, 0

    def next(self):
        r = self.i % self.n
        self.i += 1
        return r


def build_program(debug=False, stop_after=3):
    nc = bass.Bass("TRN2", target_bir_lowering=False)
    dt_in = lambda name, shape: nc.dram_tensor(name, shape, F32, kind="ExternalInput").ap()
    xT_d = dt_in("xT", [NCH, 128, 8, CH])
    kmask_d = dt_in("kmask", [128, NBLK])
    win_d = dt_in("w_in", [128, 8, DIN])
    wout_d = dt_in("w_out", [128, 8, D])
    wg_d = dt_in("w_gate", [NFG, 128, 8, 256])
    wu_d = dt_in("w_up", [NFG, 128, 8, 256])
    wd_d = dt_in("w_down", [8, 128, NFB, 128])
    vecs_d = dt_in("vecs", [128, NV])
    cst_d = dt_in("consts", [3, 128, 128])
    yT_d = nc.dram_tensor("yT", [NOWN, 128, 8, CH], F32, kind="ExternalOutput").ap()
    dk = dict(kind="ExternalOutput") if debug else {}
    kT_s = nc.dram_tensor("kT_s", [NH * 64, S], BF16, **dk).ap()
    qT_s = nc.dram_tensor("qT_s", [NH * 64, NOWN * CH], BF16, **dk).ap()
    crow_s = nc.dram_tensor("crow_s", [NH, NOWN * CH], BF16, **dk).ap()
    attn_s = nc.dram_tensor("attn_s", [DATT, NOWN * CH], BF16, **dk).ap()
    conv_s = nc.dram_tensor("conv_s", [DCONV, NOWN * CH], BF16, **dk).ap()
    v_s = nc.dram_tensor("v_s", [NBLK, 128, DATT], BF16).ap()
    wg_b = nc.dram_tensor("wg_b", [NFG, 128, 8, 256], BF16).ap()
    wu_b = nc.dram_tensor("wu_b", [NFG, 128, 8, 256], BF16).ap()
    wd_b = nc.dram_tensor("wd_b", [8, 128, NFB, 128], BF16).ap()

    sb = lambda name, shape, dt: nc.alloc_sbuf_tensor("s_" + name, shape, dt)
    vecs = sb("vecs", [128, NV], F32)
    ident = sb("ident", [128, 128], F32)
    tri = sb("tri", [128, 128], F32)
    cmask = sb("cmask", [128, 128], BF16)
    ones_bf = sb("ones_bf", [128, 128], BF16)
    ones_f = sb("ones_f", [128, 128], F32)
    epsb = sb("epsb", [128, 1], F32)
    Lall = sb("Lall", [128, NBLK, NH], F32)
    cbias = sb("cbias", [128, NBLK, NH], F32)
    pbig = nc.alloc_psum_tensor("pbig", [128, 8, 512], F32)
    pb = [pbig[:, i, :] for i in range(8)]

    def V(off, n=1):
        return vecs[:, off:off + n]

    with ExitStack() as es:
        P = Prog(nc, "p1")
        st = lambda name, shape, dt: es.enter_context(nc.sbuf_tensor("s_" + name, shape, dt))
        win = st("win", [128, 8, DIN], BF16)
        xs = [st(f"xs{i}", [128, 8, CH], F32) for i in range(2)]
        lsq = st("lsq", [128, 8, CH], BF16)
        wdiag = st("wdiag", [128, 4 * CW, 128], BF16)
        sq = st("sq", [128, 8, CH], BF16)
        hT = [st(f"hT{i}", [128, 8, CH], BF16) for i in range(2)]
        rt = st("rt", [128, CH], F32)
        rstd = st("rstd", [128, CH], F32)
        kst = [st(f"kst{i}", [128, CH], BF16) for i in range(2)]
        qst = [st(f"qst{i}", [128, CH], BF16) for i in range(2)]
        vst = [st(f"vst{i}", [128, CH], BF16) for i in range(2)]
        sg = [st(f"sg{i}", [128, CH], F32) for i in range(2)]
        ubuf = [st(f"ubuf{i}", [128, 4, HALO + CH], BF16) for i in range(2)]
        hsg = st("hsg", [128, 4, 32], F32)
        acc = st("acc", [128, 4, CH], F32)
        cvst = [st(f"cvst{i}", [128, CH], BF16) for i in range(2)]
        mean = st("mean", [128, CH], F32)
        lrt = st("lrt", [128, CH], F32)
        lrs = st("lrs", [128, CH], F32)
        m2 = st("m2", [128, CH], F32)
        tn = [st(f"tn{i}", [128, CH], F32) for i in range(4)]
        zt = st("zt", [128, 32], F32)
        et = st("et", [128, 32], F32)
        kmask = st("kmask_sb", [128, NBLK], F32)
        tot = st("tot", [128, NBLK, NH], F32)
        scA = st("scA", [128, NBLK, NH], F32)
        scB = st("scB", [128, NBLK, NH], F32)
        cst = st("cst", [32, NOWN, 128], BF16)
        ones_row = st("ones_row", [128, 8], F32)

        P.dma("sp", lambda e: e.dma_start(out=vecs[:], in_=vecs_d), writes=["vecs"], sem="c_vecs")
        P.dma("sp", lambda e: e.dma_start(out=ident[:], in_=cst_d[0]), writes=["ident"], sem="c_id")
        P.dma("sp", lambda e: e.dma_start(out=tri[:], in_=cst_d[1]), writes=["tri"], sem="c_tri")
        P.dma("sp", lambda e: e.dma_start(out=kmask[:], in_=kmask_d), writes=["kmask"], sem="c_km")
        P.dma("pool", lambda e: e.dma_start(out=cmask[:], in_=cst_d[2]), writes=["cmask"], sem="c_cm")
        for lo, hi in ((C_K, C_V), (C_V, DIN), (0, C_Q), (C_Q, C_K)):
            P.dma("pool", lambda e, lo=lo, hi=hi: e.dma_start(out=win[:, :, lo:hi], in_=win_d[:, :, lo:hi]),
                  writes=["win"], sem="c_win")
        P.op("pool", lambda e: e.memset(ones_bf[:], 1.0), writes=["ones_bf"])
        P.op("pool", lambda e: e.memset(ones_f[:], 1.0), writes=["ones_f"])
        P.op("pool", lambda e: e.memset(epsb[:], EPS), writes=["epsb"])

        ps = Rot(6)
        pend = None
        readyB = None

        def mm_group(bank, nk, lhs, rhs, out_ap, reads, wkey):
            for k in range(nk):
                P.op("pe", lambda e, k=k: e.matmul(out_ap, lhsT=lhs(k), rhs=rhs(k), start=(k == 0), stop=(k == nk - 1)),
                     reads=reads, writes=[wkey])

        def emit_norm(i):
            sl = i % 2
            X, H = xs[sl], hT[sl]
            xkey = ("xs", sl)
            P.dma("sp" if i == 0 else "pool", lambda e: e.dma_start(out=X[:], in_=xT_d[i]), writes=[xkey], sem=(xkey, i == 0))
            P.op("act", lambda e: e.activation(out=sq[:], in_=X[:], func=AF.Square), reads=[xkey], writes=["sq"])
            mm_group(7, 8, lambda k: ones_bf[:], lambda k: sq[:, k, :], pb[7][:], ["ones_bf", "sq"], ("pb", 7))
            P.op("act", lambda e: e.activation(out=rt[:], in_=pb[7][:], func=AF.Ln, bias=epsb[:, 0:1], scale=1.0 / D),
                 reads=[("pb", 7), "epsb"], writes=["rt"])
            P.op("act", lambda e: e.activation(out=rstd[:], in_=rt[:], func=AF.Exp, scale=-0.5), reads=["rt"], writes=["rstd"])
            for k in range(8):
                P.op("dve", lambda e, k=k: e.scalar_tensor_tensor(
                    out=H[:, k, :], in0=X[:, k, :], scalar=V(V_GPRE + k), in1=rstd[:], op0=ALU.mult, op1=ALU.mult),
                    reads=[xkey, "vecs", "rstd"], writes=[("hT", sl, k)])

        emit_norm(0)
        wd_todo = list(range(4 * CW))

        def wdiag_some(n):
            for _ in range(n):
                if wd_todo:
                    idx = wd_todo.pop(0)
                    P.op("dve", lambda e, idx=idx: e.tensor_scalar(out=wdiag[:, idx, :], in0=ident[:], scalar1=V(V_CW + idx), scalar2=None, op0=ALU.mult),
                         reads=["ident", "vecs"], writes=[("wdiag", idx)])
        for i in range(NCH):
            own = (i % 2 == 1)
            j = i // 2
            sl = i % 2
            X, H = xs[sl], hT[sl]
            xkey = ("xs", sl)
            U = ubuf[j % 2]
            ukh, ukm = ("ubuf_h", j % 2), ("ubuf_m", j % 2)
            hkeys = [("hT", sl, k) for k in range(8)]

            for hp in range(4):
                b = ps.next()
                c0 = C_K + hp * 128
                mm_group(b, 8, lambda k, c0=c0: win[:, k, c0:c0 + 128], lambda k, H=H: H[:, k, :], pb[b][:],
                         ["win"] + hkeys, ("pb", b))
                ks = hp % 2
                P.op("dve", lambda e, b=b, ks=ks: e.tensor_copy(out=kst[ks][:], in_=pb[b][:]),
                     reads=[("pb", b)], writes=[("kst", ks)])
                wdiag_some(6)
                P.dma("sp", lambda e, ks=ks, hp=hp, i=i: e.dma_start(
                    out=kT_s[hp * 128:(hp + 1) * 128, i * CH:(i + 1) * CH], in_=kst[ks][:]),
                    reads=[("kst", ks)], sem=("kst", ks))
            for tb in range(4):
                b = ps.next()
                mm_group(b, 8, lambda k, H=H, tb=tb: H[:, k, tb * 128:(tb + 1) * 128],
                         lambda k: win[:, k, C_V:C_V + 512], pb[b][:], ["win"] + hkeys, ("pb", b))
                blk = i * 4 + tb
                vs = tb % 2
                P.op("dve", lambda e, b=b, vs=vs: e.tensor_copy(out=vst[vs][:], in_=pb[b][:]),
                     reads=[("pb", b)], writes=[("vst", vs)])
                wdiag_some(6)
                P.dma("sp", lambda e, vs=vs, blk=blk: e.dma_start(out=v_s[blk], in_=vst[vs][:]),
                      reads=[("vst", vs)], sem=("vst", vs))
            if i + 1 < NCH:
                emit_norm(i + 1)
            for tb in range(4):
                mm_group(6, 8, lambda k, H=H, tb=tb: H[:, k, tb * 128:(tb + 1) * 128],
                         lambda k: win[:, k, C_F:C_F + 8], pb[6][:, tb * 8:(tb + 1) * 8], ["win"] + hkeys, ("pb", 6))
            P.op("dve", lambda e: e.tensor_tensor(out=zt[:], in0=pb[6][:, 0:32], in1=V(V_BF, 32), op=ALU.add),
                 reads=[("pb", 6), "vecs"], writes=["zt"])
            P.op("act", lambda e: e.activation(out=et[:], in_=zt[:], func=AF.Exp, scale=-1.0), reads=["zt"], writes=["et"])
            P.op("act", lambda e, i=i: e.activation(
                out=Lall[:, i * 4:(i + 1) * 4, :], in_=et[:].rearrange("p (a h) -> p a h", h=NH), func=AF.Ln, bias=1.0, scale=1.0),
                reads=["et"], writes=["Lall"])

            if not own:
                bh = ps.next()
                for cb in range(4):
                    for g in range(2):
                        c0 = (C_VAL if g == 0 else C_GATE) + cb * 128
                        col = (cb * 2 + g) * 32
                        mm_group(bh, 8, lambda k, c0=c0: win[:, k, c0:c0 + 128],
                                 lambda k, H=H: H[:, k, CH - HALO:CH], pb[bh][:, col:col + HALO],
                                 ["win"] + hkeys, ("pb", bh))
                hv = pb[bh][:, 0:256].rearrange("p (c g w) -> p c g w", c=4, g=2)
                P.op("act", lambda e, hv=hv: e.activation(out=hsg[:, :, 0:HALO], in_=hv[:, :, 1, 0:HALO], func=AF.Sigmoid),
                     reads=[("pb", bh)], writes=["hsg"])
                P.op("dve", lambda e, hv=hv, U=U: e.tensor_tensor(out=U[:, :, 0:HALO], in0=hv[:, :, 0, 0:HALO],
                                                                  in1=hsg[:, :, 0:HALO], op=ALU.mult),
                     reads=[("pb", bh), "hsg"], writes=[ukh])
                if pend is not None:
                    pend[0]()
                    readyB = pend[1]
                    pend = None
                continue

            for hp in range(4):
                b = ps.next()
                c0 = C_Q + hp * 128
                mm_group(b, 8, lambda k, c0=c0: win[:, k, c0:c0 + 128], lambda k, H=H: H[:, k, :], pb[b][:],
                         ["win"] + hkeys, ("pb", b))
                qs = hp % 2
                P.op("dve", lambda e, b=b, qs=qs: e.tensor_scalar(out=qst[qs][:], in0=pb[b][:], scalar1=0.125, scalar2=None, op0=ALU.mult),
                     reads=[("pb", b)], writes=[("qst", qs)])
                P.dma("sp", lambda e, qs=qs, hp=hp, j=j: e.dma_start(
                    out=qT_s[hp * 128:(hp + 1) * 128, j * CH:(j + 1) * CH], in_=qst[qs][:]),
                    reads=[("qst", qs)], sem=("qst", qs))
            for cb in range(4):
                bg = ps.next()
                c0 = C_GATE + cb * 128
                mm_group(bg, 8, lambda k, c0=c0: win[:, k, c0:c0 + 128], lambda k, H=H: H[:, k, :], pb[bg][:],
                         ["win"] + hkeys, ("pb", bg))
                s2 = cb % 2
                P.op("act", lambda e, bg=bg, s2=s2: e.activation(out=sg[s2][:], in_=pb[bg][:], func=AF.Sigmoid),
                     reads=[("pb", bg)], writes=[("sg", s2)])
                bv = ps.next()
                c0 = C_VAL + cb * 128
                mm_group(bv, 8, lambda k, c0=c0: win[:, k, c0:c0 + 128], lambda k, H=H: H[:, k, :], pb[bv][:],
                         ["win"] + hkeys, ("pb", bv))
                P.op("dve", lambda e, bv=bv, s2=s2, cb=cb, U=U: e.tensor_tensor(
                    out=U[:, cb, HALO:HALO + CH], in0=pb[bv][:], in1=sg[s2][:], op=ALU.mult),
                    reads=[("pb", bv), ("sg", s2)], writes=[(ukm, cb)])
            if readyB is not None:
                readyB()
                readyB = None

            def make_A(U=U, ukh=ukh, ukm=ukm):
                def stageA():
                    for cb in range(4):
                        b = ps.next()
                        for t in range(CW):
                            P.op("pe", lambda e, cb=cb, t=t, b=b: e.matmul(
                                pb[b][:], lhsT=wdiag[:, cb * CW + t, :], rhs=U[:, cb, t:t + CH], start=(t == 0), stop=(t == CW - 1)),
                                reads=[("wdiag", cb * CW + t), ukh, (ukm, cb)], writes=[("pb", b)])
                        P.op("dve", lambda e, cb=cb, b=b: e.tensor_scalar(
                            out=acc[:, cb, :], in0=pb[b][:], scalar1=V(V_CB + cb), scalar2=None, op0=ALU.add),
                            reads=[("pb", b), "vecs"], writes=[("acc", cb)])
                    acck = [("acc", cb) for cb in range(4)]
                    P.op("act", lambda e: e.activation(out=lsq[:, 0:4, :], in_=acc[:], func=AF.Copy), reads=acck, writes=["lsq"])
                    P.op("act", lambda e: e.activation(out=lsq[:, 4:8, :], in_=acc[:], func=AF.Square), reads=acck, writes=["lsq"])
                return stageA

            def make_B(j=j):
                def stageB():
                    b1 = ps.next()
                    mm_group(b1, 4, lambda k: ones_bf[:], lambda k: lsq[:, k, :], pb[b1][:], ["ones_bf", "lsq"], ("pb", b1))
                    b2 = ps.next()
                    mm_group(b2, 4, lambda k: ones_bf[:], lambda k: lsq[:, 4 + k, :], pb[b2][:], ["ones_bf", "lsq"], ("pb", b2))
                    P.op("act", lambda e: e.activation(out=mean[:], in_=pb[b1][:], func=AF.Copy, scale=1.0 / DCONV),
                         reads=[("pb", b1)], writes=["mean"])
                    P.op("dve", lambda e: e.tensor_tensor(out=m2[:], in0=mean[:], in1=mean[:], op=ALU.mult), reads=["mean"], writes=["m2"])
                    P.op("dve", lambda e: e.scalar_tensor_tensor(out=m2[:], in0=pb[b2][:], scalar=1.0 / DCONV, in1=m2[:],
                                                                 op0=ALU.mult, op1=ALU.subtract),
                         reads=[("pb", b2), "m2"], writes=["m2"])
                    P.op("act", lambda e: e.activation(out=lrt[:], in_=m2[:], func=AF.Ln, bias=epsb[:, 0:1], scale=1.0),
                         reads=["m2", "epsb"], writes=["lrt"])
                    P.op("act", lambda e: e.activation(out=lrs[:], in_=lrt[:], func=AF.Exp, scale=-0.5), reads=["lrt"], writes=["lrs"])
                    for cb in range(4):
                        P.op("dve", lambda e, cb=cb: e.tensor_tensor(out=tn[cb][:], in0=acc[:, cb, :], in1=mean[:], op=ALU.subtract),
                             reads=[("acc", cb), "mean"], writes=[("tn", cb)])
                    for cb in range(4):
                        P.op("dve", lambda e, cb=cb: e.tensor_tensor(out=tn[cb][:], in0=tn[cb][:], in1=lrs[:], op=ALU.mult),
                             reads=[("tn", cb), "lrs"], writes=[("tn", cb)])
                    for cb in range(4):
                        t2 = cb % 2
                        P.op("act", lambda e, cb=cb, t2=t2: e.activation(
                            out=cvst[t2][:], in_=tn[cb][:], func=AF.Silu, bias=V(V_LNB + cb), scale=V(V_LNG + cb)),
                            reads=[("tn", cb), "vecs"], writes=[("cvst", t2)])
                        P.dma("sp", lambda e, cb=cb, t2=t2: e.dma_start(
                            out=conv_s[cb * 128:(cb + 1) * 128, j * CH:(j + 1) * CH], in_=cvst[t2][:]),
                            reads=[("cvst", t2)], sem=("cvst", t2))
                return stageB

            assert pend is None
            pend = (make_A(), make_B())

        L2 = Lall[:].rearrange("p b h -> p (b h)")
        P.op("pe", lambda e: e.matmul(pb[0][:], lhsT=tri[:], rhs=L2, start=True, stop=True), reads=["tri", "Lall"], writes=[("pb", 0)])
        P.op("pe", lambda e: e.matmul(pb[1][:], lhsT=ones_f[:], rhs=L2, start=True, stop=True), reads=["ones_f", "Lall"], writes=[("pb", 1)])
        P.op("dve", lambda e: e.tensor_copy(out=tot[:].rearrange("p b h -> p (b h)"), in_=pb[1][:]), reads=[("pb", 1)], writes=["tot"])
        P.op("act", lambda e: e.activation(out=scA[:].rearrange("p b h -> p (b h)"), in_=pb[1][:], func=AF.Copy),
             reads=[("pb", 1)], writes=["scA"])
        cur, oth, ck, ok = scA, scB, "scA", "scB"
        s = 1
        while s < NBLK:
            P.op("dve", lambda e, cur=cur, oth=oth, s=s: e.tensor_copy(out=oth[:, 0:s, :], in_=cur[:, 0:s, :]), reads=[ck], writes=[ok])
            P.op("dve", lambda e, cur=cur, oth=oth, s=s: e.tensor_tensor(out=oth[:, s:NBLK, :], in0=cur[:, s:NBLK, :],
                                                                       in1=cur[:, 0:NBLK - s, :], op=ALU.add),
                 reads=[ck], writes=[ok])
            cur, oth, ck, ok = oth, cur, ok, ck
            s *= 2
        P.op("dve", lambda e, cur=cur: e.tensor_tensor(out=tot[:], in0=cur[:], in1=tot[:], op=ALU.subtract), reads=[ck, "tot"], writes=["tot"])
        P.op("dve", lambda e: e.tensor_tensor(out=tot[:].rearrange("p b h -> p (b h)"), in0=tot[:].rearrange("p b h -> p (b h)"),
                                              in1=pb[0][:], op=ALU.add), reads=["tot", ("pb", 0)], writes=["tot"])
        for h in range(NH):
            P.op("dve", lambda e, h=h: e.tensor_tensor(out=cbias[:, :, h], in0=tot[:, :, h], in1=kmask[:], op=ALU.add),
                 reads=["tot", "kmask"], writes=["cbias"])
        if pend is not None:
            pend[0]()
            pend[1]()
            pend = None

        for j in range(NOWN):
            blk0 = (2 * j + 1) * 4
            b = 2 + (j // 4)
            P.op("pe", lambda e, b=b, blk0=blk0, j=j: e.matmul(
                pb[b][0:32, (j % 4) * 128:(j % 4 + 1) * 128], lhsT=tot[:, blk0:blk0 + 4, :].rearrange("p b h -> p (b h)"),
                rhs=ident[:], start=True, stop=True),
                reads=["tot", "ident"], writes=[("pb", b)])
        for half in range(2):
            P.op("act", lambda e, half=half: e.activation(
                out=cst[:, half * 4:(half + 1) * 4, :], in_=pb[2 + half][0:32, :].rearrange("p (j t) -> p j t", j=4),
                func=AF.Copy, scale=-1.0),
                reads=[("pb", 2 + half)], writes=["cst"])
        crow_v = crow_s.rearrange("h (j bl t) -> h j bl t", j=NOWN, bl=4)
        for bl in range(4):
            P.dma("sp", lambda e, bl=bl: e.dma_start(out=crow_v[:, :, bl, :], in_=cst[bl * 8:(bl + 1) * 8, :, :]),
                  reads=["cst"], sem="cst")
        P.emit()
    if stop_after == 1:
        return nc

    wout = sb("wout", [128, 8, D], BF16)
    with ExitStack() as es:
        P = Prog(nc, "p2")
        st = lambda name, shape, dt: es.enter_context(nc.sbuf_tensor("s_" + name, shape, dt))
        kT = [st(f"kT{i}", [65, S], BF16) for i in range(2)]
        qT = [st(f"qT{i}", [65, NOWN * CH], BF16) for i in range(2)]
        Vh = [st(f"Vh{i}", [128, NBLK, 128], BF16) for i in range(2)]
        NPT = 4
        pt = [st(f"pt{i}", [128, 2, CH], BF16) for i in range(NPT)]
        osb = [st(f"osb{i}", [65, CH], F32) for i in range(2)]
        rec = [st(f"rec{i}", [65, CH], F32) for i in range(2)]
        ost = [st(f"ost{i}", [64, CH], BF16) for i in range(2)]
        rhi = [st(f"rhi{i}", [65, CH], BF16) for i in range(2)]
        rlo = [st(f"rlo{i}", [65, CH], BF16) for i in range(2)]
        fill = st("fill", [128, 256], BF16)
        P.op("pool", lambda e: e.memset(fill[:], 0.5), writes=["fill"])
        def init_slot(i):
            for g4 in range(4):
                P.op("dve", lambda e, g4=g4: e.memset(kT[i][64:65, g4 * 2048:(g4 + 1) * 2048], 1.0), writes=[("kT1", i, g4)])
            P.op("pool", lambda e: e.memset(Vh[i][:, :, 64:128], 0.0), writes=[("Vh1", i)])
            P.op("pool", lambda e: e.memset(Vh[i][:, :, 64:65], 1.0), writes=[("Vh1", i)])

        init_slot(0)
        cvt = [lambda e, fg=fg: e.dma_start(out=wg_b[fg], in_=wg_d[fg]) for fg in range(NFG)]
        cvt += [lambda e, fg=fg: e.dma_start(out=wu_b[fg], in_=wu_d[fg]) for fg in range(NFG)]
        cvt += [lambda e, ob=ob: e.dma_start(out=wd_b[ob], in_=wd_d[ob]) for ob in range(8)]
        cvt = [lambda e: e.dma_start(out=wout[:], in_=wout_d)] + cvt
        sbank = Rot(4)
        ptr = Rot(NPT)
        it = 0
        def load_head(h):
            sl = h % 2
            P.dma("sp", lambda e: e.dma_start(out=qT[sl][0:64, :], in_=qT_s[h * 64:(h + 1) * 64, :]),
                  writes=[("qT", sl)], sem=("qT", sl))
            P.dma("sp", lambda e: e.dma_start(out=qT[sl][64:65, :], in_=crow_s[h:h + 1, :]),
                  writes=[("qT", sl)], sem=("qT", sl))
            for g4 in range(4):
                P.dma("sp", lambda e, g4=g4: e.dma_start(
                    out=kT[sl][0:64, g4 * 2048:(g4 + 1) * 2048], in_=kT_s[h * 64:(h + 1) * 64, g4 * 2048:(g4 + 1) * 2048]),
                    writes=[("kT", sl, g4)], sem=("kT", sl, g4))
                P.dma("sp", lambda e, g4=g4: e.dma_start(
                    out=Vh[sl][:, g4 * 16:(g4 + 1) * 16, 0:64],
                    in_=v_s[g4 * 16:(g4 + 1) * 16, :, h * 64:(h + 1) * 64].rearrange("b p d -> p b d")),
                    writes=[("Vh", sl, g4)], sem=("Vh", sl, g4))

        pvq = []
        normB_cd = []
        half = [0]

        def pop_pv():
            t = pvq.pop(0)
            for c in range(t["n"]):
                po, qlo, first, last, norm = t["po"][c], t["qlo"], t["first"], t["last"], t["norm"][c]
                P.op("pe", lambda e, t=t, c=c, po=po, qlo=qlo, first=first, last=last: e.matmul(
                    pb[po][:, qlo:CH], lhsT=Vh[t["sl"]][:, t["kb"], :], rhs=pt[t["pi"]][:, c, qlo:CH], start=first, stop=last),
                    reads=[("Vh", t["sl"], t["kb"] // 16), ("Vh1", t["sl"]), ("pt", t["pi"])], writes=[("pb", po)])
                if last:
                    norm[0]()
                    normB_cd.append([8, norm[1]])

        def make_norm(po, o2, h, j):
            def normA():
                P.op("dve", lambda e: e.tensor_copy(out=osb[o2][:], in_=pb[po][0:65, :]),
                     reads=[("pb", po)], writes=[("osb", o2)])
                P.op("dve", lambda e: e.reciprocal(out=rec[o2][64:65, :], in_=osb[o2][64:65, :]),
                     reads=[("osb", o2)], writes=[("rec", o2)])
                P.op("dve", lambda e: e.tensor_copy(out=rhi[o2][64:65, :], in_=rec[o2][64:65, :]),
                     reads=[("rec", o2)], writes=[("rhi", o2)])
                P.op("dve", lambda e: e.tensor_tensor(out=rlo[o2][64:65, :], in0=rec[o2][64:65, :], in1=rhi[o2][64:65, :], op=ALU.subtract),
                     reads=[("rec", o2), ("rhi", o2)], writes=[("rlo", o2)])

            def normB():
                P.op("pe", lambda e: e.matmul(pb[6][0:64, :], lhsT=ones_bf[64:65, 0:64], rhs=rhi[o2][64:65, :],
                                              start=True, stop=False),
                     reads=["ones_bf", ("rhi", o2)], writes=[("pb", 6)])
                P.op("pe", lambda e: e.matmul(pb[6][0:64, :], lhsT=ones_bf[64:65, 0:64], rhs=rlo[o2][64:65, :],
                                              start=False, stop=True),
                     reads=["ones_bf", ("rlo", o2)], writes=[("pb", 6)])
                P.op("dve", lambda e: e.tensor_tensor(out=ost[o2][:], in0=osb[o2][0:64, :], in1=pb[6][0:64, :], op=ALU.mult),
                     reads=[("osb", o2), ("pb", 6)], writes=[("ost", o2)])
                P.dma("sp", lambda e: e.dma_start(
                    out=attn_s[h * 64:(h + 1) * 64, j * CH:(j + 1) * CH], in_=ost[o2][:]),
                    reads=[("ost", o2)], sem=("ost", o2))
            return normA, normB

        def emit_tile(h, sl, kb, chunks, dk_, first, last, norms):
            n = len(chunks)
            qlo = 128 * dk_ if dk_ > 0 else 0
            if n == 2 and half[0] % 2 == 1:
                half[0] += 1
            hb = half[0] % 4
            half[0] += n
            pi = ptr.next()
            for c, j in enumerate(chunks):
                P.op("pe", lambda e, c=c, j=j: e.matmul(
                    pb[hb + c][:, qlo:CH], lhsT=kT[sl][:, kb * 128:(kb + 1) * 128],
                    rhs=qT[sl][:, j * CH + qlo:(j + 1) * CH], start=True, stop=True),
                    reads=[("kT", sl, kb // 16), ("kT1", sl, kb // 16), ("qT", sl)], writes=[("pb", hb + c)])
            for _ in range(1 if n == 2 else 2):
                P.op("pe", lambda e: e.matmul(pb[7][:, 0:256], lhsT=ones_bf[:], rhs=fill[:], start=True, stop=True),
                     reads=["fill"], writes=[("pb", 7)])
            if n == 2:
                P.op("act", lambda e: e.activation(
                    out=pt[pi][:], in_=pbig[:, hb:hb + 2, :], func=AF.Exp, bias=cbias[:, kb, h:h + 1], scale=1.0),
                    reads=[("pb", hb), ("pb", hb + 1), "cbias"], writes=[("pt", pi)])
            else:
                P.op("act", lambda e: e.activation(
                    out=pt[pi][:, 0, qlo:CH], in_=pb[hb][:, qlo:CH], func=AF.Exp, bias=cbias[:, kb, h:h + 1], scale=1.0),
                    reads=[("pb", hb), "cbias"], writes=[("pt", pi)])
            if dk_ >= 0:
                assert n == 1
                P.op("pool", lambda e: e.tensor_tensor(
                    out=pt[pi][:, 0, 128 * dk_:128 * dk_ + 128], in0=pt[pi][:, 0, 128 * dk_:128 * dk_ + 128], in1=cmask[:], op=ALU.mult),
                    reads=[("pt", pi), "cmask"], writes=[("pt", pi)])
            pvq.append(dict(n=n, sl=sl, kb=kb, pi=pi, qlo=qlo, first=first, last=last,
                            po=[4 + (j % 2) for j in chunks], norm=norms))
            if normB_cd:
                normB_cd[0][0] -= 1
                if normB_cd[0][0] <= 0:
                    normB_cd.pop(0)[1]()
            if len(pvq) > 2:
                pop_pv()

        load_head(0)
        for h in range(NH):
            sl = h % 2
            if h + 1 < NH:
                while pvq:
                    pop_pv()
                load_head(h + 1)
            if h >= 1:
                for _ in range(6):
                    if cvt:
                        P.dma("pool", cvt.pop(0), sem="cvt")
            for pr in range(NOWN // 2):
                if h == 0 and pr == 1:
                    init_slot(1)
                j0, j1 = 2 * pr, 2 * pr + 1
                da, db = 8 * j0 + 4, 8 * j1 + 4
                n0 = make_norm(4, 0, h, j0)
                n1 = make_norm(5, 1, h, j1)
                emit_tile(h, sl, 0, [j0, j1], -1, True, False, [n0, n1])
                for kb in range(1, da):
                    emit_tile(h, sl, kb, [j0, j1], -1, False, False, [n0, n1])
                for dk_ in (3, 2, 1):
                    emit_tile(h, sl, da + dk_, [j0], dk_, False, False, [n0])
                emit_tile(h, sl, da, [j0], 0, False, True, [n0])
                for dk_ in (3, 2, 1):
                    emit_tile(h, sl, db + dk_, [j1], dk_, False, False, [n1])
                for kb in range(da, db):
                    emit_tile(h, sl, kb, [j1], -1, False, False, [n1])
                emit_tile(h, sl, db, [j1], 0, False, True, [n1])
        while pvq:
            pop_pv()
        while normB_cd:
            normB_cd.pop(0)[1]()
        assert not cvt
        P.emit()
    if stop_after == 2:
        return nc

    with ExitStack() as es:
        P = Prog(nc, "p3")
        st = lambda name, shape, dt: es.enter_context(nc.sbuf_tensor("s_" + name, shape, dt))
        xs = [st(f"x3_{i}", [128, 8, CH], F32) for i in range(2)]
        at = [st(f"at{i}", [128, 4, CH], BF16) for i in range(2)]
        cv = [st(f"cv{i}", [128, 4, CH], BF16) for i in range(2)]
        mix = st("mix", [128, 8, CH], F32)
        yb = st("yb", [128, 8, CH], F32)
        sq = st("sq3", [128, 8, CH], BF16)
        h2 = [st(f"h2_{i}", [128, 8, CH], BF16) for i in range(2)]
        act = st("actT", [128, NFB, CH], BF16)
        rt = st("rt3", [128, CH], F32)
        rstd = st("rstd3", [128, CH], F32)
        tmp = [st(f"tmp{i}", [128, CH], F32) for i in range(4)]
        sgt = [st(f"sgt{i}", [128, CH], F32) for i in range(2)]
        NWB = 3
        wg = [st(f"wg{i}", [128, 8, 256], BF16) for i in range(NWB)]
        wu = [st(f"wu{i}", [128, 8, 256], BF16) for i in range(NWB)]
        wd = [st(f"wd{i}", [128, NFB, 128], BF16) for i in range(NWB)]
        ps = Rot(6)
        wgr, wdr = Rot(NWB), Rot(NWB)
        xkeys = lambda sl: [("x3", sl, k) for k in range(8)]
        mk = [("mix", ob) for ob in range(8)]
        yk = [("yb", ob) for ob in range(8)]

        def load_chunk(j):
            sl = j % 2
            P.dma("sp", lambda e: e.dma_start(
                out=at[sl][:], in_=attn_s[:, j * CH:(j + 1) * CH].rearrange("(c p) t -> p c t", p=128)),
                writes=[("at", sl)], sem=("at", sl))
            P.dma("sp", lambda e: e.dma_start(
                out=cv[sl][:], in_=conv_s[:, j * CH:(j + 1) * CH].rearrange("(c p) t -> p c t", p=128)),
                writes=[("cv", sl)], sem=("cv", sl))
            P.dma("pool", lambda e: e.dma_start(out=xs[sl][:], in_=xT_d[2 * j + 1]), writes=xkeys(sl), sem=("x3", sl))

        def square(src, srckeys):
            P.op("act", lambda e: e.activation(out=sq[:], in_=src[:], func=AF.Square), reads=srckeys, writes=["sq"])

        def stats():
            for k in range(8):
                P.op("pe", lambda e, k=k: e.matmul(pb[7][:], lhsT=ones_bf[:], rhs=sq[:, k, :], start=(k == 0), stop=(k == 7)),
                     reads=["ones_bf", "sq"], writes=[("pb", 7)])
            P.op("act", lambda e: e.activation(out=rt[:], in_=pb[7][:], func=AF.Ln, bias=epsb[:, 0:1], scale=1.0 / D),
                 reads=[("pb", 7), "epsb"], writes=["rt"])
            P.op("act", lambda e: e.activation(out=rstd[:], in_=rt[:], func=AF.Exp, scale=-0.5), reads=["rt"], writes=["rstd"])

        def resid_apply(src, srckey, goff, sl):
            X = xs[sl]
            for g in range(2):
                for ob in range(g * 4, g * 4 + 4):
                    t2 = ob % 4
                    P.op("dve", lambda e, ob=ob, t2=t2: e.tensor_tensor(out=tmp[t2][:], in0=src[:, ob, :], in1=rstd[:], op=ALU.mult),
                         reads=[(srckey, ob), "rstd"], writes=[("tmp", t2)])
                for ob in range(g * 4, g * 4 + 4):
                    t2 = ob % 4
                    P.op("dve", lambda e, ob=ob, t2=t2: e.scalar_tensor_tensor(
                        out=X[:, ob, :], in0=tmp[t2][:], scalar=V(goff + ob), in1=X[:, ob, :], op0=ALU.mult, op1=ALU.add),
                        reads=[("tmp", t2), ("x3", sl, ob), "vecs"], writes=[("x3", sl, ob)])

        def A1(j):
            sl = j % 2
            A, Cv = at[sl], cv[sl]
            for ob in range(8):
                b = ps.next()
                for k in range(8):
                    rhs = (lambda k=k: Cv[:, k, :]) if k < 4 else (lambda k=k: A[:, k - 4, :])
                    P.op("pe", lambda e, k=k, b=b, ob=ob, rhs=rhs: e.matmul(
                        pb[b][:], lhsT=wout[:, k, ob * 128:(ob + 1) * 128], rhs=rhs(), start=(k == 0), stop=(k == 7)),
                        reads=["wout", ("at", sl), ("cv", sl)], writes=[("pb", b)])
                P.op("act", lambda e, b=b, ob=ob: e.activation(out=mix[:, ob, :], in_=pb[b][:], func=AF.Copy),
                     reads=[("pb", b)], writes=[("mix", ob)])
            square(mix, mk)

        def A2(j):
            sl = j % 2
            stats()
            resid_apply(mix, "mix", V_GPOST, sl)

        def A2b(j):
            sl = j % 2
            square(xs[sl], xkeys(sl))

        def A3(j):
            sl = j % 2
            X, H2 = xs[sl], h2[sl]
            stats()
            for k in range(8):
                P.op("dve", lambda e, k=k: e.scalar_tensor_tensor(
                    out=H2[:, k, :], in0=X[:, k, :], scalar=V(V_GFPRE + k), in1=rstd[:], op0=ALU.mult, op1=ALU.mult),
                    reads=[("x3", sl, k), "rstd", "vecs"], writes=[("h2", sl, k)])

        def up_group(j, fg):
            sl = j % 2
            H2 = h2[sl]
            hk = [("h2", sl, k) for k in range(8)]
            ws = wgr.next()
            P.dma("sp", lambda e: e.dma_start(out=wg[ws][:], in_=wg_b[fg]), writes=[("wg", ws)], sem=("wg", ws))
            P.dma("sp", lambda e: e.dma_start(out=wu[ws][:], in_=wu_b[fg]), writes=[("wu", ws)], sem=("wu", ws))
            for fb2 in range(2):
                fb = fg * 2 + fb2
                bg = ps.next()
                for k in range(8):
                    P.op("pe", lambda e, k=k, bg=bg, fb2=fb2: e.matmul(
                        pb[bg][:], lhsT=wg[ws][:, k, fb2 * 128:(fb2 + 1) * 128], rhs=H2[:, k, :], start=(k == 0), stop=(k == 7)),
                        reads=[("wg", ws)] + hk, writes=[("pb", bg)])
                s2 = fb % 2
                P.op("act", lambda e, bg=bg, s2=s2: e.activation(out=sgt[s2][:], in_=pb[bg][:], func=AF.Silu),
                     reads=[("pb", bg)], writes=[("sgt", s2)])
                bu = ps.next()
                for k in range(8):
                    P.op("pe", lambda e, k=k, bu=bu, fb2=fb2: e.matmul(
                        pb[bu][:], lhsT=wu[ws][:, k, fb2 * 128:(fb2 + 1) * 128], rhs=H2[:, k, :], start=(k == 0), stop=(k == 7)),
                        reads=[("wu", ws)] + hk, writes=[("pb", bu)])
                P.op("dve", lambda e, bu=bu, s2=s2, fb=fb: e.tensor_tensor(out=act[:, fb, :], in0=pb[bu][:], in1=sgt[s2][:], op=ALU.mult),
                     reads=[("pb", bu), ("sgt", s2)], writes=[("act", fb)])

        def down(j):
            ak = [("act", fb) for fb in range(NFB)]
            for ob in range(8):
                ws = wdr.next()
                P.dma("sp", lambda e, ws=ws, ob=ob: e.dma_start(out=wd[ws][:], in_=wd_b[ob]), writes=[("wd", ws)], sem=("wd", ws))
                b = ps.next()
                for fb in range(NFB):
                    P.op("pe", lambda e, fb=fb, b=b, ws=ws: e.matmul(
                        pb[b][:], lhsT=wd[ws][:, fb, :], rhs=act[:, fb, :], start=(fb == 0), stop=(fb == NFB - 1)),
                        reads=[("wd", ws)] + ak, writes=[("pb", b)])
                P.op("act", lambda e, b=b, ob=ob: e.activation(out=yb[:, ob, :], in_=pb[b][:], func=AF.Copy),
                     reads=[("pb", b)], writes=[("yb", ob)])
            square(yb, yk)

        def N2(j):
            sl = j % 2
            stats()
            resid_apply(yb, "yb", V_GFPOST, sl)
            P.dma("pool", lambda e: e.dma_start(out=yT_d[j], in_=xs[sl][:]), reads=xkeys(sl), sem=("x3", sl))

        load_chunk(0)
        A1(0)
        A2(0)
        A2b(0)
        A3(0)
        for j in range(NOWN):
            for fg in range(NFG):
                up_group(j, fg)
                if fg == 0:
                    if j > 0:
                        N2(j - 1)
                    if j + 1 < NOWN:
                        load_chunk(j + 1)
                if j + 1 < NOWN:
                    if fg == 2:
                        A1(j + 1)
                    elif fg == 4:
                        A2(j + 1)
                    elif fg == 7:
                        A2b(j + 1)
                    elif fg == 9:
                        A3(j + 1)
            down(j)
        N2(NOWN - 1)
        P.emit()
    return nc


_NC_CACHE = {}


def _host_inputs(x, g_mix_pre, w_in, b_forget, conv_w, conv_b, conv_ln_g, conv_ln_b,
                 w_out, g_mix_post, g_ffn_pre, w_gate, w_up, w_down, g_ffn_post):
    f32 = np.float32
    x = np.asarray(x, f32)
    pk = lambda v, n: np.asarray(v, f32).reshape(n, 128).T
    vecs = np.zeros((128, NV), f32)
    vecs[:, V_GPRE:V_GPRE + 8] = pk(g_mix_pre[0], 8)
    vecs[:, V_GPOST:V_GPOST + 8] = pk(g_mix_post[0], 8)
    vecs[:, V_GFPRE:V_GFPRE + 8] = pk(g_ffn_pre[0], 8)
    vecs[:, V_GFPOST:V_GFPOST + 8] = pk(g_ffn_post[0], 8)
    vecs[:, V_CB:V_CB + 4] = pk(conv_b[0], 4)
    vecs[:, V_LNG:V_LNG + 4] = pk(conv_ln_g[0], 4)
    vecs[:, V_LNB:V_LNB + 4] = pk(conv_ln_b[0], 4)
    cw = np.asarray(conv_w[0], f32)
    vecs[:, V_CW:V_CW + 4 * CW] = cw.reshape(CW, 4, 128).transpose(2, 1, 0).reshape(128, 4 * CW)
    vecs[:, V_BF:V_BF + 32] = np.tile(np.asarray(b_forget[0], f32), 4)[None, :]
    ar = np.arange(128)
    consts = np.stack([np.eye(128, dtype=f32),
                       (ar[:, None] <= ar[None, :]).astype(f32),
                       (ar[:, None] <= ar[None, :]).astype(f32)])
    lay_kp = lambda w: np.ascontiguousarray(np.asarray(w, f32).reshape(8, 128, -1).transpose(1, 0, 2))
    win_h = lay_kp(w_in[0])
    wout_h = lay_kp(w_out[0])
    grp = lambda w: np.ascontiguousarray(lay_kp(w).reshape(128, 8, NFG, 256).transpose(2, 0, 1, 3))
    wg_h, wu_h = grp(w_gate[0]), grp(w_up[0])
    wd_h = np.ascontiguousarray(np.asarray(w_down[0], f32).reshape(NFB, 128, 8, 128).transpose(2, 1, 0, 3))
    shared = {"w_in": win_h, "w_out": wout_h, "w_gate": wg_h, "w_up": wu_h, "w_down": wd_h,
              "vecs": vecs, "consts": consts}
    in_maps = []
    for c in range(8):
        b, p = c // 2, c % 2
        xb = x[b].reshape(NCH, CH, 8, 128).transpose(0, 3, 2, 1)
        xT = np.zeros((NCH, 128, 8, CH), f32)
        km = np.zeros((128, NBLK), f32)
        if p == 1:
            xT[:] = xb
        else:
            xT[1:] = xb[:NCH - 1]
            km[:, 0:4] = -30000.0
        m = dict(shared)
        m["xT"] = xT
        m["kmask"] = km
        in_maps.append(m)
    return in_maps


def kernel(**inputs):
    if "nc" not in _NC_CACHE:
        _NC_CACHE["nc"] = build_program()
    nc = _NC_CACHE["nc"]
    in_maps = _host_inputs(**inputs)
    res = run_bass_kernel_spmd(nc, in_maps, core_ids=list(range(8)))
    out = np.zeros((NB, S, D), np.float32)
    for c in range(8):
        b, p = c // 2, c % 2
        yT = np.asarray(res.results[c]["yT"], np.float32)
        y = yT.transpose(0, 3, 2, 1).reshape(NOWN, CH, D)
        ov = out[b].reshape(NCH, CH, D)
        ov[p::2] = y
    return out
```
